# Optimizing a Trainium2 kernel written in Bass

```python
import math
import jax, jax.numpy as jnp
from jax import lax
import numpy as np

D_MODEL = 1024
BATCH = 8
SEQ = 2048
DEPTH = 2

CONV_WIDTH = D_MODEL // 2
CONV_GROUPS = 8
CONV_K = 3
N_HEADS = 8
HEAD_DIM = 64
ATTN_WIDTH = N_HEADS * HEAD_DIM
D_FF = 2816
Q_BLOCK = 128
RMS_EPS = 1e-6
N_BRANCH = 2
IN_COLS = 3 * CONV_WIDTH + 3 * ATTN_WIDTH + N_BRANCH * D_MODEL

kernel_name = "hybrid_shortconv_stickbreaking_macaron"


def rmsnorm(x, g):
    xf = x.astype(jnp.float32)
    y = xf * lax.rsqrt(jnp.mean(xf * xf, axis=-1, keepdims=True) + RMS_EPS)
    return (y * g.astype(jnp.float32)).astype(x.dtype)


def swiglu(h, w13, w2):
    gu = h @ w13
    a, b = jnp.split(gu, 2, axis=-1)
    return (jax.nn.silu(a) * b) @ w2


def causal_depthwise_conv(u, w):
    c = u.shape[-1]
    return lax.conv_general_dilated(
        u, w[:, None, :].astype(u.dtype), window_strides=(1,), padding=[(CONV_K - 1, 0)],
        dimension_numbers=("NWC", "WIO", "NWC"), feature_group_count=c)


def stick_breaking_attention(q, k, v):
    b, s, h, dh = q.shape
    q = jnp.transpose(q, (0, 2, 1, 3))
    k = jnp.transpose(k, (0, 2, 1, 3))
    v = jnp.transpose(v, (0, 2, 1, 3))
    scale = 1.0 / math.sqrt(dh)
    outs = []
    for i in range(s // Q_BLOCK):
        q0 = i * Q_BLOCK
        end = q0 + Q_BLOCK
        qb = q[:, :, q0:end]
        kb = k[:, :, :end]
        vb = v[:, :, :end]
        z = jnp.einsum("bhqd,bhkd->bhqk", qb, kb).astype(jnp.float32) * scale
        t_pos = q0 + jnp.arange(Q_BLOCK)[:, None]
        s_pos = jnp.arange(end)[None, :]
        causal = s_pos < t_pos
        log_keep = jnp.where(causal, jax.nn.log_sigmoid(-z), 0.0)
        shifted = jnp.concatenate([log_keep[..., 1:], jnp.zeros_like(log_keep[..., :1])], axis=-1)
        excl = lax.cumsum(shifted, axis=shifted.ndim - 1, reverse=True)
        log_a = jax.nn.log_sigmoid(z) + excl
        a = jnp.where(causal, jnp.exp(log_a), 0.0)
        outs.append(jnp.einsum("bhqk,bhkd->bhqd", a.astype(vb.dtype), vb))
    o = jnp.concatenate(outs, axis=2)
    return jnp.transpose(o, (0, 2, 1, 3)).reshape(b, s, h * dh)


def setup_inputs(seed: int = 0) -> dict:
    key = jax.random.key(seed)
    ks = jax.random.split(key, 16)
    f32 = jnp.float32

    def w(k, shape, fan_in):
        return jax.random.normal(k, shape, f32) * (fan_in ** -0.5)

    def gain(k, shape):
        return 1.0 + 0.02 * jax.random.normal(k, shape, f32)

    return {
        "x": jax.random.normal(ks[0], (BATCH, SEQ, D_MODEL), f32),
        "ffn1_norm": gain(ks[1], (DEPTH, D_MODEL)),
        "ffn1_w13": w(ks[2], (DEPTH, D_MODEL, 2 * D_FF), D_MODEL),
        "ffn1_w2": w(ks[3], (DEPTH, D_FF, D_MODEL), D_FF),
        "mix_norm": gain(ks[4], (DEPTH, D_MODEL)),
        "w_in": w(ks[5], (DEPTH, D_MODEL, IN_COLS), D_MODEL),
        "b_gate": 0.01 * jax.random.normal(ks[6], (DEPTH, N_BRANCH * D_MODEL), f32),
        "conv_w": w(ks[7], (DEPTH, CONV_K, CONV_WIDTH), CONV_K),
        "w_conv_o": w(ks[8], (DEPTH, CONV_WIDTH, D_MODEL), CONV_WIDTH),
        "w_attn_o": w(ks[9], (DEPTH, ATTN_WIDTH, D_MODEL), ATTN_WIDTH),
        "w_out": w(ks[10], (DEPTH, D_MODEL, D_MODEL), D_MODEL),
        "ffn2_norm": gain(ks[11], (DEPTH, D_MODEL)),
        "ffn2_w13": w(ks[12], (DEPTH, D_MODEL, 2 * D_FF), D_MODEL),
        "ffn2_w2": w(ks[13], (DEPTH, D_FF, D_MODEL), D_FF),
        "final_norm": gain(ks[14], (D_MODEL,)),
    }


def reference(x, ffn1_norm, ffn1_w13, ffn1_w2, mix_norm, w_in, b_gate, conv_w,
              w_conv_o, w_attn_o, w_out, ffn2_norm, ffn2_w13, ffn2_w2, final_norm):
    b, s, _ = x.shape
    splits = [CONV_WIDTH, 2 * CONV_WIDTH, 3 * CONV_WIDTH,
              3 * CONV_WIDTH + ATTN_WIDTH, 3 * CONV_WIDTH + 2 * ATTN_WIDTH,
              3 * CONV_WIDTH + 3 * ATTN_WIDTH]
    for l in range(DEPTH):
        x = x + 0.5 * swiglu(rmsnorm(x, ffn1_norm[l]), ffn1_w13[l], ffn1_w2[l])

        h = rmsnorm(x, mix_norm[l])
        proj = h @ w_in[l]
        cb, cc, cx, q, k, v, gl = jnp.split(proj, splits, axis=-1)
        gates = jax.nn.sigmoid(gl + b_gate[l])
        g_conv, g_attn = jnp.split(gates, 2, axis=-1)

        conv_y = cb * causal_depthwise_conv(cc * cx, conv_w[l])
        conv_branch = conv_y @ w_conv_o[l]

        attn_y = stick_breaking_attention(q.reshape(b, s, N_HEADS, HEAD_DIM),
                                          k.reshape(b, s, N_HEADS, HEAD_DIM),
                                          v.reshape(b, s, N_HEADS, HEAD_DIM))
        attn_branch = attn_y @ w_attn_o[l]

        merged = g_conv * conv_branch + g_attn * attn_branch
        x = x + merged @ w_out[l]

        x = x + 0.5 * swiglu(rmsnorm(x, ffn2_norm[l]), ffn2_w13[l], ffn2_w2[l])
    return rmsnorm(x, final_norm)
```

```python
import numpy as np
from contextlib import ExitStack
import concourse.bass as bass
import concourse.mybir as mybir
from concourse.bass_utils import run_bass_kernel_spmd

F32 = mybir.dt.float32
BF16 = mybir.dt.bfloat16
AF = mybir.ActivationFunctionType
ALU = mybir.AluOpType

D = 1024
DFF = 2816
NFH = 11
EPS = 1e-6
NSLOT = 6
NEG = -30000.0
NWARM = 1

BIG_NAMES = ("aT", "qT", "kp", "v", "cy", "oT", "mg", "xs", "yT", "os")


class Sched:
    def __init__(self, nc, dry=False):
        self.nc = nc
        self.dry = dry
        self.ops = []
        self.lastw = {}
        self.readers = {}
        self.alias_deps = {}
        self.ecount = {}

    def op(self, eng, fn, r=(), w=(), dma=None):
        if self.dry:
            return
        i = len(self.ops)
        deps = set()
        for k in r:
            p = self.lastw.get(k)
            if p is not None:
                deps.add(p)
        for k in w:
            p = self.lastw.get(k)
            if p is not None:
                deps.add(p)
            for q in self.readers.get(k, ()):
                deps.add(q)
            ad = self.alias_deps.get(k[0])
            if ad:
                deps |= ad
        eidx = self.ecount.get(eng, 0)
        self.ecount[eng] = eidx + 1
        best = {}
        keep = []
        for d in deps:
            P = self.ops[d]
            if P["dma"] is not None:
                keep.append(d)
            else:
                b = best.get(P["eng"])
                if b is None or d > b:
                    best[P["eng"]] = d
        keep.extend(best.values())
        self.ops.append(dict(eng=eng, fn=fn, dma=dma, deps=keep, eidx=eidx, sig=False, sigval=0))
        for k in r:
            lst = self.readers.setdefault(k, [])
            if dma is None:
                lst[:] = [q for q in lst if self.ops[q]["dma"] is not None or self.ops[q]["eng"] != eng]
            lst.append(i)
        for k in w:
            self.lastw[k] = i
            self.readers[k] = []

    def barrier(self, new_names, old_names=BIG_NAMES):
        if self.dry:
            return
        dset = set()
        for k, p in self.lastw.items():
            if k[0] in old_names:
                dset.add(p)
        for k, lst in self.readers.items():
            if k[0] in old_names:
                dset.update(lst)
        best = {}
        keep = set()
        for d in dset:
            P = self.ops[d]
            if P["dma"] is not None:
                keep.add(d)
            else:
                b = best.get(P["eng"])
                if b is None or d > b:
                    best[P["eng"]] = d
        keep.update(best.values())
        for n in new_names:
            self.alias_deps[n] = set(keep)

    def emit(self, stack, final_waits_engine="sp"):
        nc = self.nc
        engobj = {"pe": nc.tensor, "act": nc.scalar, "dve": nc.vector, "pool": nc.gpsimd, "sp": nc.sync}
        ops = self.ops
        for op in ops:
            waits = []
            for d in op["deps"]:
                P = ops[d]
                if P["dma"] is not None:
                    waits.append(d)
                    continue
                if P["eng"] == op["eng"] and op["dma"] is None:
                    if op["eng"] == "pe":
                        continue
                P["sig"] = True
                waits.append(d)
            op["waits"] = waits
        cnt = {}
        dcnt = {}
        for op in ops:
            if op["dma"] is not None:
                dcnt[op["dma"]] = dcnt.get(op["dma"], 0) + 16
                op["sigval"] = dcnt[op["dma"]]
            elif op["sig"]:
                cnt[op["eng"]] = cnt.get(op["eng"], 0) + 1
                op["sigval"] = cnt[op["eng"]]
        esem = {e: stack.enter_context(nc.semaphore("s_" + e)) for e in ("pe", "act", "dve", "pool")}
        dsem = {}
        for k in dcnt:
            dsem[k] = stack.enter_context(nc.semaphore("d_" + "_".join(str(x) for x in k)))
        waited = {e: {} for e in engobj}
        for op in ops:
            E = engobj[op["eng"]]
            wd = waited[op["eng"]]
            for d in sorted(op["waits"]):
                P = ops[d]
                if P["dma"] is not None:
                    key = ("d", P["dma"])
                    sem = dsem[P["dma"]]
                else:
                    key = ("e", P["eng"])
                    sem = esem[P["eng"]]
                if wd.get(key, 0) >= P["sigval"]:
                    continue
                E.wait_ge(sem, P["sigval"])
                wd[key] = P["sigval"]
            ins = op["fn"]()
            if op["dma"] is not None:
                ins.then_inc(dsem[op["dma"]], 16)
            elif op["sig"]:
                ins.then_inc(esem[op["eng"]], 1)
        E = engobj[final_waits_engine]
        for k, v in dcnt.items():
            if k[0] == "os":
                E.wait_ge(dsem[k], v)
        return len(ops)


class WStream:
    def __init__(self, nc, sch, slots, order=None):
        self.nc = nc
        self.sch = sch
        self.slots = slots
        self.record = order is None
        self.order = [] if order is None else order
        self.issued = 0
        self.cur = 0

    def _issue_to(self, n):
        nc = self.nc
        while self.issued < min(n, len(self.order)):
            i = self.issued
            ap, nk = self.order[i]
            s = i % NSLOT
            dst = self.slots[s][:, 0:nk, :]

            def fn(dst=dst, ap=ap):
                return nc.gpsimd.dma_start(out=dst, in_=ap)
            self.sch.op("pool", fn, r=(), w=[("w", s)], dma=("w", s))
            self.issued += 1

    def next(self, ap, nk):
        if self.record:
            self.order.append((ap, nk))
            i = len(self.order) - 1
        else:
            i = self.cur
            self.cur += 1
            self._issue_to(i + NSLOT - 3)
        s = i % NSLOT
        return self.slots[s], ("w", s)


def emit_program(nc, T, sch, ws, S, depth):
    NG = S // 512
    NT = S // 128
    xT, hT = T["xT"], T["hT"]
    ps = T["ps"]
    ppair = T["pp"]
    zerosW = T["cbf"][:, 640:768]
    vecs = T["vecs"]
    ident = T["ident"]
    cb = T["cbf"]
    negTri = cb[:, 0:128]
    negOnes = cb[:, 128:256]
    onesD = cb[:, 256:384]
    identb = cb[:, 384:512]
    negmask = [cb[:, 512:640]]
    CK = [("cbf",)]

    st = dict(ps=0, psacc=0, w32=0, wbf=0, alt=0, psmod=6)

    def psalloc():
        b = st["ps"] % st["psmod"]
        st["ps"] += 1
        return b

    pending = []

    def defer(fn, lag=2):
        pending.append([lag, fn])

    def tick():
        for it in pending:
            it[0] -= 1
        while pending and pending[0][0] <= 0:
            pending.pop(0)[1]()

    def flush():
        while pending:
            pending.pop(0)[1]()

    def stats_after_update(dc, g, next_gcol=None):
        sq, sqk = wbf()
        ACT(sq[:], xT[:, dc, gsl(g)], AF.Square, xk([dc], g), [sqk])

        def fn():
            MM(ps[4 + g][:], onesD, sq[:], dc == 0, dc == 7, [sqk] + CK, [("ps", 4 + g)])
            if dc == 7 and next_gcol is not None:
                norm_group(g, next_gcol, True)
        defer(fn)

    def psacc():
        b = 6 + st["psacc"] % 2
        st["psacc"] += 1
        return b

    ew, bw = T["ew"], T["bw"]

    def w32():
        i = st["w32"] % (2 * len(ew))
        st["w32"] += 1
        return ew[i // 2][:, (i % 2) * 512:(i % 2 + 1) * 512], ("ew", i // 2, i % 2)

    def wbf():
        i = st["wbf"] % (2 * len(bw))
        st["wbf"] += 1
        return bw[i // 2][:, (i % 2) * 512:(i % 2 + 1) * 512], ("bw", i // 2, i % 2)

    def wide(pool, name):
        cnt = "c_" + name
        i = st.setdefault(cnt, 0) % len(pool)
        st[cnt] += 1
        return pool[i], [(name, i, 0), (name, i, 1)]

    def v3(t):
        return t[:, :].rearrange("p (h c) -> p h c", h=2)

    def MM(out, lhsT, rhs, start, stop, r, w):
        sch.op("pe", lambda: nc.tensor.matmul(out, lhsT, rhs, start=start, stop=stop), r, w)

    def ACT(out, in_, func, r, w, bias=None, scale=None):
        kw = {}
        if bias is not None:
            kw["bias"] = bias
        if scale is not None:
            kw["scale"] = scale
        sch.op("act", lambda: nc.scalar.activation(out=out, in_=in_, func=func, **kw), r, w)

    def VEC(eng):
        return nc.vector if eng == "dve" else nc.gpsimd

    def TT(eng, out, in0, in1, op, r, w):
        sch.op(eng, lambda: VEC(eng).tensor_tensor(out=out, in0=in0, in1=in1, op=op), r, w)

    def STT(eng, out, in0, scalar, in1, op0, op1, r, w):
        sch.op(eng, lambda: VEC(eng).scalar_tensor_tensor(out=out, in0=in0, scalar=scalar, in1=in1, op0=op0, op1=op1), r, w)

    def TS1(eng, out, in0, scalar, op, r, w):
        sch.op(eng, lambda: VEC(eng).tensor_scalar(out=out, in0=in0, scalar1=scalar, scalar2=None, op0=op), r, w)

    def COPY(eng, out, in_, r, w):
        if eng == "act":
            sch.op("act", lambda: nc.scalar.activation(out=out, in_=in_, func=AF.Copy), r, w)
        else:
            sch.op(eng, lambda: VEC(eng).tensor_copy(out=out, in_=in_), r, w)

    def MEMSET(eng, ap, val, w):
        sch.op(eng, lambda: VEC(eng).memset(ap, val), (), w)

    def xk(kcs, g):
        return [("xT", kc, tt) for kc in kcs for tt in range(4 * g, 4 * g + 4)]

    def gsl(g):
        return slice(g * 512, (g + 1) * 512)

    sch.op("sp", lambda: nc.sync.dma_start(out=vecs[:], in_=T["d_vecs"][:, :]), (), [("vecs",)], dma=("c", 0))
    sch.op("sp", lambda: nc.sync.dma_start(out=ident[:], in_=T["d_ident"][:, :]), (), [("ident",)], dma=("c", 1))
    sch.op("pool", lambda: nc.gpsimd.dma_start(out=cb[:], in_=T["d_cbf"][:, :]), (), CK, dma=("c", 2))

    def phase0():
        xs = T["xs"]
        st["psmod"] = 4
        for tt in range(NT):
            g = tt // 4
            xb_ = tt % len(xs)
            sbuf = xs[xb_]
            src = T["d_x"][tt * 128:(tt + 1) * 128, :]
            sch.op("sp", lambda sbuf=sbuf, src=src: nc.sync.dma_start(out=sbuf[:], in_=src), (), [("xs", xb_)],
                   dma=("xs", xb_))
            for half in range(2):
                b = psalloc()
                for i in range(4):
                    kc = half * 4 + i
                    MM(ps[b][:, i * 128:(i + 1) * 128], sbuf[:, kc * 128:(kc + 1) * 128], ident[:], True, True,
                       [("xs", xb_), ("ident",)], [("ps", b)])
                eng = "act"
                xkeys = [("xT", kc, tt) for kc in range(half * 4, half * 4 + 4)]
                COPY(eng, xT[:, half * 4:half * 4 + 4, tt * 128:(tt + 1) * 128],
                     ps[b][:, :].rearrange("p (a b) -> p a b", a=4), [("ps", b)], xkeys)

                sq, sqk = wbf()
                ACT(sq[:, :].rearrange("p (a b) -> p a b", a=4), xT[:, half * 4:half * 4 + 4, tt * 128:(tt + 1) * 128],
                    AF.Square, xkeys, [sqk])

                def fn(tt=tt, half=half, g=g, sq=sq, sqk=sqk):
                    tl = tt % 4
                    for i in range(4):
                        MM(ps[4 + g][:, tl * 128:(tl + 1) * 128], onesD, sq[:, i * 128:(i + 1) * 128],
                           half == 0 and i == 0, half == 1 and i == 3, [sqk] + CK, [("ps", 4 + g)])
                defer(fn, 3)
                tick()
            if tt % 4 == 3:
                flush()
                norm_group(g, 0, True)

    def rms_stats(g, pre=False):
        if pre:
            b = 4 + g
        else:
            b = psalloc()
            for kc in range(8):
                sq, sqk = wbf()
                ACT(sq[:], xT[:, kc, gsl(g)], AF.Square, xk([kc], g), [sqk])
                MM(ps[b][:], onesD, sq[:], kc == 0, kc == 7, [sqk] + CK, [("ps", b)])
        rt, rk = w32()
        ACT(rt[:], ps[b][:], AF.Sqrt, [("ps", b), ("epsb",)], [rk], bias=T["epsb"][:, 0:1])
        sch.op("dve", lambda: nc.vector.reciprocal(out=rt[:], in_=rt[:]), [rk], [rk])
        return rt, rk

    def norm_group(g, gcol, pre):
        rt, rk = rms_stats(g, pre)
        for kc in range(8):
            STT("dve", hT[:, kc, gsl(g)], xT[:, kc, gsl(g)], vecs[:, gcol + kc:gcol + kc + 1], rt[:],
                ALU.mult, ALU.mult, xk([kc], g) + [rk, ("vecs",)], [("hT", kc, g)])

    def ffn(w13, w2, next_gcol):
        aT = T["aT"]
        sch.barrier(["aT"])
        st["psmod"] = 6
        w13v = w13.rearrange("(kc p) c -> p kc c", p=128)
        w2v = w2.rearrange("(fc p) c -> p fc c", p=128)
        for half in range(2):
            for fi in range(NFH):
                f = half * NFH + fi
                wa, wak = ws.next(w13v[:, :, f * 128:(f + 1) * 128], 8)
                wb, wbk = ws.next(w13v[:, :, DFF + f * 128:DFF + (f + 1) * 128], 8)
                for g in range(NG):
                    pa = psalloc()
                    pb = psalloc()
                    for kc in range(8):
                        MM(ps[pa][:], wa[:, kc, :], hT[:, kc, gsl(g)], kc == 0, kc == 7, [wak, ("hT", kc, g)], [("ps", pa)])
                    for kc in range(8):
                        MM(ps[pb][:], wb[:, kc, :], hT[:, kc, gsl(g)], kc == 0, kc == 7, [wbk, ("hT", kc, g)], [("ps", pb)])
                    s, sk = w32()
                    ACT(s[:], ps[pa][:], AF.Silu, [("ps", pa)], [sk])
                    TT("dve", aT[:, fi, gsl(g)], s[:], ps[pb][:], ALU.mult, [sk, ("ps", pb)], [("aT", fi, g)])
            if half == 1:
                st["psmod"] = 4
            for dc in range(8):
                w2a, w2ak = ws.next(w2v[:, half * NFH:half * NFH + 6, dc * 128:(dc + 1) * 128], 6)
                w2b, w2bk = ws.next(w2v[:, half * NFH + 6:(half + 1) * NFH, dc * 128:(dc + 1) * 128], NFH - 6)
                for g in range(NG):
                    po = psalloc()
                    for fi in range(NFH):
                        wt_, wk_ = (w2a[:, fi, :], w2ak) if fi < 6 else (w2b[:, fi - 6, :], w2bk)
                        MM(ps[po][:], wt_, aT[:, fi, gsl(g)], fi == 0, fi == NFH - 1,
                           [wk_, ("aT", fi, g)], [("ps", po)])
                    STT("dve", xT[:, dc, gsl(g)], ps[po][:], 0.5, xT[:, dc, gsl(g)], ALU.mult, ALU.add,
                        [("ps", po)] + xk([dc], g), xk([dc], g))
                    if half == 1:
                        stats_after_update(dc, g, next_gcol)
                    tick()
            if half == 1:
                flush()

    def mixer(l, vb, next_gcol):
        qT, kp, vv, cy, oT, mg = T["qT"], T["kp"], T["v"], T["cy"], T["oT"], T["mg"]
        ub = T["ub"]
        sch.barrier(["qT", "kp", "v", "cy", "oT"])
        gcol = vb + 8
        bgc = vb + 24
        cwc = vb + 40
        st["psmod"] = 8
        winv = T["d_w_in"][l].rearrange("(kc p) c -> p kc c", p=128)
        for buf in range(2):
            for (lo, hi, hh_) in ((64, 128, 0), (0, 64, 1)):
                sch.op("act", lambda o=kp[buf][lo:hi, hh_, :], i=hT[lo:hi, 0, :]: nc.scalar.mul(out=o, in_=i, mul=0.0),
                       [("hT", 0, g) for g in range(NG)], [("kp", buf, hh_, g, 1 - hh_) for g in range(NG)])

        ubc = 0
        for j in range(4):
            wcb, wcbk = ws.next(winv[:, :, j * 128:(j + 1) * 128], 8)
            wcc, wcck = ws.next(winv[:, :, 512 + j * 128:512 + (j + 1) * 128], 8)
            wcx, wcxk = ws.next(winv[:, :, 1024 + j * 128:1024 + (j + 1) * 128], 8)
            prev = None
            for g in range(NG):
                pcb, pcc, pcx = psalloc(), psalloc(), psalloc()
                for (pp, wt, wk_) in ((pcb, wcb, wcbk), (pcc, wcc, wcck), (pcx, wcx, wcxk)):
                    for kc in range(8):
                        MM(ps[pp][:], wt[:, kc, :], hT[:, kc, gsl(g)], kc == 0, kc == 7, [wk_, ("hT", kc, g)], [("ps", pp)])
                cxs, cxk = w32()
                COPY("act", cxs[:], ps[pcx][:], [("ps", pcx)], [cxk])
                u = ub[ubc % 2]
                uk = ("ub", ubc % 2)
                ukc = ("ubc", ubc % 2)
                ubc += 1
                if prev is None:
                    MEMSET("pool", u[:, 0:2], 0.0, [ukc])
                else:
                    COPY("pool", u[:, 0:2], prev[0][:, 512:514], [prev[1]], [ukc])
                TT("dve", u[:, 2:514], ps[pcc][:], cxs[:], ALU.mult, [("ps", pcc), cxk], [uk])
                prev = (u, uk)
                y, yk = w32()
                c0 = cwc + 0 * 4 + j
                c1 = cwc + 1 * 4 + j
                c2 = cwc + 2 * 4 + j
                TS1("dve", y[:], u[:, 0:512], vecs[:, c0:c0 + 1], ALU.mult, [uk, ukc, ("vecs",)], [yk])
                STT("dve", y[:], u[:, 1:513], vecs[:, c1:c1 + 1], y[:], ALU.mult, ALU.add, [uk, ukc, yk, ("vecs",)], [yk])
                STT("dve", y[:], u[:, 2:514], vecs[:, c2:c2 + 1], y[:], ALU.mult, ALU.add, [uk, yk, ("vecs",)], [yk])
                TT("dve", cy[:, j, gsl(g)], y[:], ps[pcb][:], ALU.mult, [yk, ("ps", pcb)], [("cy", j, g)])

        def a_qkv(hp, buf):
            wq, wqk = ws.next(winv[:, :, 1536 + hp * 128:1536 + (hp + 1) * 128], 8)
            wkk, wkkk = ws.next(winv[:, :, 2048 + hp * 128:2048 + (hp + 1) * 128], 8)
            wv, wvk = ws.next(winv[:, :, 2560 + hp * 128:2560 + (hp + 1) * 128], 8)
            for g in range(NG):
                pq, pk = psalloc(), psalloc()
                for kc in range(8):
                    MM(ps[pq][:], wq[:, kc, :], hT[:, kc, gsl(g)], kc == 0, kc == 7, [wqk, ("hT", kc, g)], [("ps", pq)])
                for kc in range(8):
                    MM(ps[pk][:], wkk[:, kc, :], hT[:, kc, gsl(g)], kc == 0, kc == 7, [wkkk, ("hT", kc, g)], [("ps", pk)])
                sch.op("act", lambda o=qT[buf][:, gsl(g)], i=ps[pq][:]: nc.scalar.mul(out=o, in_=i, mul=0.125), [("ps", pq)], [("qT", buf, g)])
                COPY("dve", kp[buf][0:64, 0, gsl(g)], ps[pk][0:64, :], [("ps", pk)], [("kp", buf, 0, g, 0)])
                COPY("dve", kp[buf][64:128, 1, gsl(g)], ps[pk][64:128, :], [("ps", pk)], [("kp", buf, 1, g, 1)])
            for t4 in range(NT // 4):
                pv = psalloc()
                for i in range(4):
                    tt = t4 * 4 + i
                    for kc in range(8):
                        MM(ps[pv][:, i * 128:(i + 1) * 128], hT[:, kc, tt * 128:(tt + 1) * 128], wv[:, kc, :],
                           kc == 0, kc == 7, [wvk, ("hT", kc, t4)], [("ps", pv)])
                COPY("dve", vv[buf][:, t4 * 4:t4 * 4 + 4, :], ps[pv][:, :].rearrange("p (a b) -> p a b", a=4),
                     [("ps", pv)], [("v", buf, t4)])

        rpool = T["rp"]

        def attn(hp, buf):
            blocks = []
            for qt in range(NG):
                nblk = 4 * qt + 4
                grp = dict(R=None)
                for bi, sb in enumerate(reversed(range(nblk))):
                    blocks.append(dict(qt=qt, sb=sb, bi=bi, nblk=nblk, grp=grp, idx=len(blocks)))

            def geom(B):
                qt, sb = B["qt"], B["sb"]
                r = sb - 4 * qt
                c0 = max(r, 0) * 128
                return r, c0, slice(c0, 512), slice(qt * 512 + c0, (qt + 1) * 512), slice(sb * 128, (sb + 1) * 128)

            def kkeys(B, hh):
                return [("kp", buf, hh, B["sb"] // 4, 0), ("kp", buf, hh, B["sb"] // 4, 1), ("qT", buf, B["qt"])]

            def S1a(B):
                r, c0, cs, qs, ksl = geom(B)
                for hh in range(2):
                    for j in range(NWARM):
                        MM(ps[hh][:, cs], zerosW, qT[buf][:, qs], j == 0, False, [("qT", buf, B["qt"])] + CK, [("ps", hh)])
                    MM(ps[hh][:, cs], kp[buf][:, hh, ksl], qT[buf][:, qs], NWARM == 0, r < 0, kkeys(B, hh), [("ps", hh)])
                    if r >= 0:
                        MM(ps[hh][:, c0:c0 + 128], identb, negmask[0][:, 0:128], False, True, CK, [("ps", hh)])
                e, ek = wide(ew, "ew")
                ACT(v3(e)[:, :, cs], v3(ppair[0])[:, :, cs], AF.Exp, [("ps", 0), ("ps", 1)], ek)
                B["e"], B["ek"] = e, ek

            def S1b(B):
                r, c0, cs, qs, ksl = geom(B)
                e, ek = B["e"], B["ek"]
                lt, lk = wide(bw, "bw")
                if c0 > 0:
                    MEMSET("pool", v3(lt)[:, :, 0:c0], 0.0, lk)
                ACT(v3(lt)[:, :, cs], v3(e)[:, :, cs], AF.Ln, ek + [("oneb",)], lk, bias=T["oneb"][:, 0:1])
                B["lt"], B["lk"] = lt, lk
                g = B["grp"]
                B["Rin"] = g["R"]
                if B["sb"] > 0:
                    if g["R"] is None:
                        g["R"] = (lt, lk)
                    else:
                        Rn, Rnk = wide(rpool, "rp")
                        TT("dve", Rn[:, :], g["R"][0][:, :], lt[:, :], ALU.add, g["R"][1] + lk, Rnk)
                        g["R"] = (Rn, Rnk)

            def S2a(B):
                r, c0, cs, qs, ksl = geom(B)
                pcp = 1 + (B["idx"] % 2)
                Rin = B["Rin"]
                for hh in range(2):
                    b = 2 * pcp + hh
                    MM(ps[b][:, cs], kp[buf][:, hh, ksl], qT[buf][:, qs], True, False, kkeys(B, hh), [("ps", b)])
                    if r >= 0:
                        MM(ps[b][:, c0:c0 + 128], identb, negmask[0][:, 0:128], False, False, CK, [("ps", b)])
                    MM(ps[b][:, cs], negTri, v3(B["lt"])[:, hh, cs], False, Rin is None, B["lk"] + CK, [("ps", b)])
                    if Rin is not None:
                        MM(ps[b][:, cs], negOnes, v3(Rin[0])[:, hh, cs], False, True, Rin[1] + CK, [("ps", b)])
                B["pcp"] = pcp

            def S2b(B):
                r, c0, cs, qs, ksl = geom(B)
                pcp = B["pcp"]
                a, ak = wide(bw, "bw")
                if B["bi"] == 0 and c0 > 0:
                    MEMSET("pool", v3(a)[:, :, 0:c0], 0.0, ak)
                ACT(v3(a)[:, :, cs], v3(ppair[pcp])[:, :, cs], AF.Exp, [("ps", 2 * pcp), ("ps", 2 * pcp + 1)], ak)
                B["a"], B["ak"] = a, ak

            def S3(B):
                r, c0, cs, qs, ksl = geom(B)
                qt, sb, bi, nblk = B["qt"], B["sb"], B["bi"], B["nblk"]
                if bi == 0:
                    cs = slice(0, 512)
                for hh in range(2):
                    MM(ps[6 + hh][:, cs], vv[buf][:, sb, :], v3(B["a"])[:, hh, cs], bi == 0, bi == nblk - 1,
                       B["ak"] + [("v", buf, sb // 4)], [("ps", 6 + hh)])
                if bi == nblk - 1:
                    for hh in range(2):
                        po = hh * 64
                        COPY("dve", oT[po:po + 64, hp, gsl(qt)], ps[6 + hh][po:po + 64, :], [("ps", 6 + hh)],
                             [("oT", hp, qt, hh)])

            n = len(blocks)
            for i in range(n + 3):
                if i < n:
                    S1a(blocks[i])
                if 0 <= i - 1 < n:
                    S2a(blocks[i - 1])
                if 0 <= i - 2 < n:
                    S2b(blocks[i - 2])
                if i < n:
                    S1b(blocks[i])
                if 0 <= i - 3 < n:
                    S3(blocks[i - 3])

        a_qkv(0, 0)
        for hp in range(4):
            if hp + 1 < 4:
                a_qkv(hp + 1, (hp + 1) % 2)
            attn(hp, hp % 2)

        sch.barrier(["mg"], old_names=("qT", "kp", "v"))
        woc = T["d_w_conv_o"][l].rearrange("(kc p) c -> p kc c", p=128)
        woa = T["d_w_attn_o"][l].rearrange("(kc p) c -> p kc c", p=128)
        for dc in range(8):
            wgc, wgck = ws.next(winv[:, :, 3072 + dc * 128:3072 + (dc + 1) * 128], 8)
            wga, wgak = ws.next(winv[:, :, 4096 + dc * 128:4096 + (dc + 1) * 128], 8)
            wco, wcok = ws.next(woc[:, :, dc * 128:(dc + 1) * 128], 4)
            wao, waok = ws.next(woa[:, :, dc * 128:(dc + 1) * 128], 4)
            for g in range(NG):
                pgc, pga, pcb_, pab = psalloc(), psalloc(), psalloc(), psalloc()
                for kc in range(8):
                    MM(ps[pgc][:], wgc[:, kc, :], hT[:, kc, gsl(g)], kc == 0, kc == 7, [wgck, ("hT", kc, g)], [("ps", pgc)])
                for kc in range(8):
                    MM(ps[pga][:], wga[:, kc, :], hT[:, kc, gsl(g)], kc == 0, kc == 7, [wgak, ("hT", kc, g)], [("ps", pga)])
                for j in range(4):
                    MM(ps[pcb_][:], wco[:, j, :], cy[:, j, gsl(g)], j == 0, j == 3, [wcok, ("cy", j, g)], [("ps", pcb_)])
                for j in range(4):
                    MM(ps[pab][:], wao[:, j, :], oT[:, j, gsl(g)], j == 0, j == 3,
                       [waok, ("oT", j, g, 0), ("oT", j, g, 1)], [("ps", pab)])
                gc, gck = w32()
                ga, gak = w32()
                ACT(gc[:], ps[pgc][:], AF.Sigmoid, [("ps", pgc), ("vecs",)], [gck], bias=vecs[:, bgc + dc:bgc + dc + 1])
                ACT(ga[:], ps[pga][:], AF.Sigmoid, [("ps", pga), ("vecs",)], [gak], bias=vecs[:, bgc + 8 + dc:bgc + 8 + dc + 1])
                TT("dve", gc[:], gc[:], ps[pcb_][:], ALU.mult, [gck, ("ps", pcb_)], [gck])
                TT("dve", ga[:], ga[:], ps[pab][:], ALU.mult, [gak, ("ps", pab)], [gak])
                TT("dve", mg[:, dc, gsl(g)], gc[:], ga[:], ALU.add, [gck, gak], [("mg", dc, g)])
        wov = T["d_w_out"][l].rearrange("(kc p) c -> p kc c", p=128)
        st["psmod"] = 4
        for dc in range(8):
            wo, wok = ws.next(wov[:, :, dc * 128:(dc + 1) * 128], 8)
            for g in range(NG):
                po = psalloc()
                for kc in range(8):
                    MM(ps[po][:], wo[:, kc, :], mg[:, kc, gsl(g)], kc == 0, kc == 7, [wok, ("mg", kc, g)], [("ps", po)])
                TT("dve", xT[:, dc, gsl(g)], ps[po][:], xT[:, dc, gsl(g)], ALU.add, [("ps", po)] + xk([dc], g), xk([dc], g))
                stats_after_update(dc, g, next_gcol)
                tick()
        flush()

    phase0()
    for l in range(depth):
        vb = l * 52
        ffn(T["d_ffn1_w13"][l], T["d_ffn1_w2"][l], vb + 8)
        mixer(l, vb, vb + 16)
        ffn(T["d_ffn2_w13"][l], T["d_ffn2_w2"][l], (l + 1) * 52 if l + 1 < depth else None)

    sch.barrier(["yT", "os"])
    st["psmod"] = 4
    yT, osb = T["yT"], T["os"]
    fcol = depth * 52
    def fin_norm(g):
        rt, rk = rms_stats(g, True)
        yb = yT[g % 2]
        for kc in range(8):
            STT("dve", yb[:, kc, :], xT[:, kc, gsl(g)], vecs[:, fcol + kc:fcol + kc + 1], rt[:], ALU.mult, ALU.mult,
                xk([kc], g) + [rk, ("vecs",)], [("yT", g % 2, kc)])

    fin_norm(0)
    for g in range(NG):
        if g + 1 < NG:
            fin_norm(g + 1)
        yb = yT[g % 2]
        for ti in range(4):
            tt = 4 * g + ti
            so = osb[tt % 2]
            for half in range(2):
                b = psalloc()
                for i in range(4):
                    kc = half * 4 + i
                    MM(ps[b][:, i * 128:(i + 1) * 128], yb[:, kc, ti * 128:(ti + 1) * 128], ident[:], True, True,
                       [("yT", g % 2, kc), ("ident",)], [("ps", b)])
                COPY("act", so[:, half * 512:(half + 1) * 512], ps[b][:], [("ps", b)], [("os", tt % 2, half)])
            dst = T["d_out"][tt * 128:(tt + 1) * 128, :]
            sch.op("sp", lambda so=so, dst=dst: nc.sync.dma_start(out=dst, in_=so[:]),
                   [("os", tt % 2, 0), ("os", tt % 2, 1)], [], dma=("os", tt % 2))


def build(S=2048, depth=2):
    nc = bass.Bass("TRN2", target_bir_lowering=False)
    T = {}

    def din(name, shape):
        return nc.dram_tensor(name, list(shape), F32, kind="ExternalInput").ap()

    T["d_x"] = din("x", [S, D])
    T["d_ffn1_w13"] = din("ffn1_w13", [depth, D, 2 * DFF])
    T["d_ffn1_w2"] = din("ffn1_w2", [depth, DFF, D])
    T["d_w_in"] = din("w_in", [depth, D, 5120])
    T["d_w_conv_o"] = din("w_conv_o", [depth, 512, D])
    T["d_w_attn_o"] = din("w_attn_o", [depth, 512, D])
    T["d_w_out"] = din("w_out", [depth, D, D])
    T["d_ffn2_w13"] = din("ffn2_w13", [depth, D, 2 * DFF])
    T["d_ffn2_w2"] = din("ffn2_w2", [depth, DFF, D])
    NV = depth * 52 + 8
    T["d_vecs"] = din("vecs", [128, NV])
    T["d_ident"] = din("ident", [128, 128])
    T["d_cbf"] = din("cbf", [128, 768])
    T["d_out"] = nc.dram_tensor("out", [S, D], F32, kind="ExternalOutput").ap()

    NT = S // 128
    with ExitStack() as stack:
        def sb(name, shape, dt):
            return stack.enter_context(nc.sbuf_tensor(name, list(shape), dt))

        T["xT"] = sb("xT", [128, 8, S], F32)
        T["hT"] = sb("hT", [128, 8, S], BF16)
        bigw = max(NFH * S * 2, 32 * S, 40960) // 4
        big = sb("big", [128, bigw], F32)
        bigb = big.bitcast(BF16)

        def bview(off_bytes, shape):
            n = int(np.prod(shape[1:]))
            o = off_bytes // 2
            ap = bigb[:, o:o + n]
            if len(shape) == 3:
                ap = ap.rearrange("p (a b) -> p a b", a=shape[1])
            return ap

        def fview(off_bytes, shape):
            n = int(np.prod(shape[1:]))
            o = off_bytes // 4
            ap = big[:, o:o + n]
            if len(shape) == 3:
                ap = ap.rearrange("p (a b) -> p a b", a=shape[1])
            return ap

        T["aT"] = bview(0, [128, NFH, S])
        per = 2 * S + 4 * S + 2 * S
        T["qT"] = [bview(b * per, [128, S]) for b in range(2)]
        T["kp"] = [bview(b * per + 2 * S, [128, 2, S]) for b in range(2)]
        T["v"] = [bview(b * per + 6 * S, [128, NT, 128]) for b in range(2)]
        T["mg"] = bview(0, [128, 8, S])
        T["cy"] = bview(16 * S, [128, 4, S])
        T["oT"] = bview(24 * S, [128, 4, S])
        T["xs"] = [fview(b * 4096, [128, 1024]) for b in range(6)]
        T["yT"] = [fview(b * 16384, [128, 8, 512]) for b in range(2)]
        T["os"] = [fview(32768 + b * 4096, [128, 1024]) for b in range(2)]

        T["ew"] = [sb(f"ew_{i}", [128, 1024], F32) for i in range(3)]
        T["bw"] = [sb(f"bw_{i}", [128, 1024], BF16) for i in range(5)]
        T["rp"] = [sb(f"rp_{i}", [128, 1024], BF16) for i in range(3)]
        T["ub"] = [sb(f"ub{i}", [128, 514], F32) for i in range(2)]
        slots = [sb(f"wslot{i}", [128, 8, 128], BF16) for i in range(NSLOT)]
        T["vecs"] = sb("vecs_sb", [128, NV], F32)
        T["ident"] = sb("ident_sb", [128, 128], F32)
        T["cbf"] = sb("cbf_sb", [128, 768], BF16)
        T["epsb"] = sb("epsb", [128, 1], F32)
        T["oneb"] = sb("oneb", [128, 1], F32)
        T["pp"] = [stack.enter_context(nc.psum_tensor(f"pp{i}", [128, 1024], F32)) for i in range(4)]
        T["ps"] = [T["pp"][i // 2][:, (i % 2) * 512:(i % 2 + 1) * 512] for i in range(8)]

        dry = Sched(nc, dry=True)
        rec = WStream(nc, dry, slots)
        emit_program(nc, T, dry, rec, S, depth)
        sch = Sched(nc)
        sch.op("dve", lambda: nc.vector.memset(T["epsb"][:], EPS), (), [("epsb",)])
        sch.op("dve", lambda: nc.vector.memset(T["oneb"][:], 1.0), (), [("oneb",)])
        ws = WStream(nc, sch, slots, order=rec.order)
        emit_program(nc, T, sch, ws, S, depth)
        n = sch.emit(stack)
    return nc, n


def host_consts():
    ident = np.eye(128, dtype=np.float32)
    j = np.arange(128)[:, None]
    s = np.arange(128)[None, :]
    negtri = np.where(j >= s, -1.0, 0.0).astype(np.float32)
    negones = -np.ones((128, 128), np.float32)
    onesd = np.full((128, 128), 1.0 / D, np.float32)
    i = np.arange(128)[:, None]
    c = np.arange(128)[None, :]
    mask = np.where(c <= i, NEG, 0.0).astype(np.float32)
    cbf = np.concatenate([negtri, negones, onesd, ident, mask, np.zeros((128, 128), np.float32)], axis=1)
    return ident, np.ascontiguousarray(cbf)


def pack_vecs(inp, depth):
    cols = []

    def pm(v):
        return np.asarray(v, np.float32).reshape(-1, 128).T

    for l in range(depth):
        cols.append(pm(inp["ffn1_norm"][l]))
        cols.append(pm(inp["mix_norm"][l]))
        cols.append(pm(inp["ffn2_norm"][l]))
        cols.append(pm(inp["b_gate"][l]))
        cw = np.asarray(inp["conv_w"][l], np.float32)
        cols.append(np.concatenate([pm(cw[k]) for k in range(3)], axis=1))
    cols.append(pm(inp["final_norm"]))
    return np.ascontiguousarray(np.concatenate(cols, axis=1))


_CACHE = {}


def kernel(**inputs):
    x = np.asarray(inputs["x"], np.float32)
    B, S, _ = x.shape
    depth = int(np.asarray(inputs["ffn1_w13"]).shape[0])
    key = (S, depth)
    if key not in _CACHE:
        _CACHE[key] = build(S, depth)[0]
    nc = _CACHE[key]
    ident, cbf = host_consts()
    vecs = pack_vecs(inputs, depth)
    shared = {
        "ffn1_w13": np.ascontiguousarray(inputs["ffn1_w13"], np.float32),
        "ffn1_w2": np.ascontiguousarray(inputs["ffn1_w2"], np.float32),
        "w_in": np.ascontiguousarray(inputs["w_in"], np.float32),
        "w_conv_o": np.ascontiguousarray(inputs["w_conv_o"], np.float32),
        "w_attn_o": np.ascontiguousarray(inputs["w_attn_o"], np.float32),
        "w_out": np.ascontiguousarray(inputs["w_out"], np.float32),
        "ffn2_w13": np.ascontiguousarray(inputs["ffn2_w13"], np.float32),
        "ffn2_w2": np.ascontiguousarray(inputs["ffn2_w2"], np.float32),
        "vecs": vecs, "ident": ident, "cbf": cbf,
    }
    in_maps = []
    for b in range(B):
        m = dict(shared)
        m["x"] = np.ascontiguousarray(x[b])
        in_maps.append(m)
    res = run_bass_kernel_spmd(nc, in_maps, core_ids=list(range(B)))
    return np.stack([np.asarray(r["out"], np.float32) for r in res.results], axis=0)
```

```python
import numpy as np
from contextlib import ExitStack
import concourse.bass as bass
import concourse.mybir as mybir
from concourse.bass_utils import run_bass_kernel_spmd

F32 = mybir.dt.float32
BF16 = mybir.dt.bfloat16
AF = mybir.ActivationFunctionType
ALU = mybir.AluOpType

D = 1024
DFF = 2816
NFH = 11
EPS = 1e-6
NSLOT = 6
NEG = -30000.0
NWARM = 1

BIG_NAMES = ("aT", "qT", "kp", "v", "cy", "oT", "mg", "xs", "yT", "os")


class Sched:
    def __init__(self, nc, dry=False):
        self.nc = nc
        self.dry = dry
        self.ops = []
        self.lastw = {}
        self.readers = {}
        self.alias_deps = {}
        self.ecount = {}

    def op(self, eng, fn, r=(), w=(), dma=None):
        if self.dry:
            return
        i = len(self.ops)
        deps = set()
        for k in r:
            p = self.lastw.get(k)
            if p is not None:
                deps.add(p)
        for k in w:
            p = self.lastw.get(k)
            if p is not None:
                deps.add(p)
            for q in self.readers.get(k, ()):
                deps.add(q)
            ad = self.alias_deps.get(k[0])
            if ad:
                deps |= ad
        eidx = self.ecount.get(eng, 0)
        self.ecount[eng] = eidx + 1
        best = {}
        keep = []
        for d in deps:
            P = self.ops[d]
            if P["dma"] is not None:
                keep.append(d)
            else:
                b = best.get(P["eng"])
                if b is None or d > b:
                    best[P["eng"]] = d
        keep.extend(best.values())
        self.ops.append(dict(eng=eng, fn=fn, dma=dma, deps=keep, eidx=eidx, sig=False, sigval=0))
        for k in r:
            lst = self.readers.setdefault(k, [])
            if dma is None:
                lst[:] = [q for q in lst if self.ops[q]["dma"] is not None or self.ops[q]["eng"] != eng]
            lst.append(i)
        for k in w:
            self.lastw[k] = i
            self.readers[k] = []

    def barrier(self, new_names, old_names=BIG_NAMES):
        if self.dry:
            return
        dset = set()
        for k, p in self.lastw.items():
            if k[0] in old_names:
                dset.add(p)
        for k, lst in self.readers.items():
            if k[0] in old_names:
                dset.update(lst)
        best = {}
        keep = set()
        for d in dset:
            P = self.ops[d]
            if P["dma"] is not None:
                keep.add(d)
            else:
                b = best.get(P["eng"])
                if b is None or d > b:
                    best[P["eng"]] = d
        keep.update(best.values())
        for n in new_names:
            self.alias_deps[n] = set(keep)

    def emit(self, stack, final_waits_engine="sp"):
        nc = self.nc
        engobj = {"pe": nc.tensor, "act": nc.scalar, "dve": nc.vector, "pool": nc.gpsimd, "sp": nc.sync}
        ops = self.ops
        for op in ops:
            waits = []
            for d in op["deps"]:
                P = ops[d]
                if P["dma"] is not None:
                    waits.append(d)
                    continue
                if P["eng"] == op["eng"] and op["dma"] is None:
                    if op["eng"] == "pe":
                        continue
                P["sig"] = True
                waits.append(d)
            op["waits"] = waits
        cnt = {}
        dcnt = {}
        for op in ops:
            if op["dma"] is not None:
                dcnt[op["dma"]] = dcnt.get(op["dma"], 0) + 16
                op["sigval"] = dcnt[op["dma"]]
            elif op["sig"]:
                cnt[op["eng"]] = cnt.get(op["eng"], 0) + 1
                op["sigval"] = cnt[op["eng"]]
        esem = {e: stack.enter_context(nc.semaphore("s_" + e)) for e in ("pe", "act", "dve", "pool")}
        dsem = {}
        for k in dcnt:
            dsem[k] = stack.enter_context(nc.semaphore("d_" + "_".join(str(x) for x in k)))
        waited = {e: {} for e in engobj}
        for op in ops:
            E = engobj[op["eng"]]
            wd = waited[op["eng"]]
            for d in sorted(op["waits"]):
                P = ops[d]
                if P["dma"] is not None:
                    key = ("d", P["dma"])
                    sem = dsem[P["dma"]]
                else:
                    key = ("e", P["eng"])
                    sem = esem[P["eng"]]
                if wd.get(key, 0) >= P["sigval"]:
                    continue
                E.wait_ge(sem, P["sigval"])
                wd[key] = P["sigval"]
            ins = op["fn"]()
            if op["dma"] is not None:
                ins.then_inc(dsem[op["dma"]], 16)
            elif op["sig"]:
                ins.then_inc(esem[op["eng"]], 1)
        E = engobj[final_waits_engine]
        for k, v in dcnt.items():
            if k[0] == "os":
                E.wait_ge(dsem[k], v)
        return len(ops)


class WStream:
    def __init__(self, nc, sch, slots, order=None):
        self.nc = nc
        self.sch = sch
        self.slots = slots
        self.record = order is None
        self.order = [] if order is None else order
        self.issued = 0
        self.cur = 0

    def _issue_to(self, n):
        nc = self.nc
        while self.issued < min(n, len(self.order)):
            i = self.issued
            ap, nk = self.order[i]
            s = i % NSLOT
            dst = self.slots[s][:, 0:nk, :]

            def fn(dst=dst, ap=ap):
                return nc.gpsimd.dma_start(out=dst, in_=ap)
            self.sch.op("pool", fn, r=(), w=[("w", s)], dma=("w", s))
            self.issued += 1

    def next(self, ap, nk):
        if self.record:
            self.order.append((ap, nk))
            i = len(self.order) - 1
        else:
            i = self.cur
            self.cur += 1
            self._issue_to(i + NSLOT - 3)
        s = i % NSLOT
        return self.slots[s], ("w", s)


def emit_program(nc, T, sch, ws, S, depth):
    NG = S // 512
    NT = S // 128
    xT, hT = T["xT"], T["hT"]
    ps = T["ps"]
    ppair = T["pp"]
    zerosW = T["cbf"][:, 640:768]
    vecs = T["vecs"]
    ident = T["ident"]
    cb = T["cbf"]
    negTri = cb[:, 0:128]
    negOnes = cb[:, 128:256]
    onesD = cb[:, 256:384]
    identb = cb[:, 384:512]
    negmask = [cb[:, 512:640]]
    CK = [("cbf",)]

    st = dict(ps=0, psacc=0, w32=0, wbf=0, alt=0, psmod=6)

    def psalloc():
        b = st["ps"] % st["psmod"]
        st["ps"] += 1
        return b

    pending = []

    def defer(fn, lag=2):
        pending.append([lag, fn])

    def tick():
        for it in pending:
            it[0] -= 1
        while pending and pending[0][0] <= 0:
            pending.pop(0)[1]()

    def flush():
        while pending:
            pending.pop(0)[1]()

    def stats_after_update(dc, g, next_gcol=None):
        sq, sqk = wbf()
        ACT(sq[:], xT[:, dc, gsl(g)], AF.Square, xk([dc], g), [sqk])

        def fn():
            MM(ps[4 + g][:], onesD, sq[:], dc == 0, dc == 7, [sqk] + CK, [("ps", 4 + g)])
            if dc == 7 and next_gcol is not None:
                norm_group(g, next_gcol, True)
        defer(fn)

    def psacc():
        b = 6 + st["psacc"] % 2
        st["psacc"] += 1
        return b

    ew, bw = T["ew"], T["bw"]

    def w32():
        i = st["w32"] % (2 * len(ew))
        st["w32"] += 1
        return ew[i // 2][:, (i % 2) * 512:(i % 2 + 1) * 512], ("ew", i // 2, i % 2)

    def wbf():
        i = st["wbf"] % (2 * len(bw))
        st["wbf"] += 1
        return bw[i // 2][:, (i % 2) * 512:(i % 2 + 1) * 512], ("bw", i // 2, i % 2)

    def wide(pool, name):
        cnt = "c_" + name
        i = st.setdefault(cnt, 0) % len(pool)
        st[cnt] += 1
        return pool[i], [(name, i, 0), (name, i, 1)]

    def v3(t):
        return t[:, :].rearrange("p (h c) -> p h c", h=2)

    def MM(out, lhsT, rhs, start, stop, r, w):
        sch.op("pe", lambda: nc.tensor.matmul(out, lhsT, rhs, start=start, stop=stop), r, w)

    def ACT(out, in_, func, r, w, bias=None, scale=None):
        kw = {}
        if bias is not None:
            kw["bias"] = bias
        if scale is not None:
            kw["scale"] = scale
        sch.op("act", lambda: nc.scalar.activation(out=out, in_=in_, func=func, **kw), r, w)

    def VEC(eng):
        return nc.vector if eng == "dve" else nc.gpsimd

    def TT(eng, out, in0, in1, op, r, w):
        sch.op(eng, lambda: VEC(eng).tensor_tensor(out=out, in0=in0, in1=in1, op=op), r, w)

    def STT(eng, out, in0, scalar, in1, op0, op1, r, w):
        sch.op(eng, lambda: VEC(eng).scalar_tensor_tensor(out=out, in0=in0, scalar=scalar, in1=in1, op0=op0, op1=op1), r, w)

    def TS1(eng, out, in0, scalar, op, r, w):
        sch.op(eng, lambda: VEC(eng).tensor_scalar(out=out, in0=in0, scalar1=scalar, scalar2=None, op0=op), r, w)

    def COPY(eng, out, in_, r, w):
        if eng == "act":
            sch.op("act", lambda: nc.scalar.activation(out=out, in_=in_, func=AF.Copy), r, w)
        else:
            sch.op(eng, lambda: VEC(eng).tensor_copy(out=out, in_=in_), r, w)

    def MEMSET(eng, ap, val, w):
        sch.op(eng, lambda: VEC(eng).memset(ap, val), (), w)

    def xk(kcs, g):
        return [("xT", kc, tt) for kc in kcs for tt in range(4 * g, 4 * g + 4)]

    def gsl(g):
        return slice(g * 512, (g + 1) * 512)

    sch.op("sp", lambda: nc.sync.dma_start(out=vecs[:], in_=T["d_vecs"][:, :]), (), [("vecs",)], dma=("c", 0))
    sch.op("sp", lambda: nc.sync.dma_start(out=ident[:], in_=T["d_ident"][:, :]), (), [("ident",)], dma=("c", 1))
    sch.op("pool", lambda: nc.gpsimd.dma_start(out=cb[:], in_=T["d_cbf"][:, :]), (), CK, dma=("c", 2))

    def phase0():
        xs = T["xs"]
        st["psmod"] = 4
        for tt in range(NT):
            g = tt // 4
            xb_ = tt % len(xs)
            sbuf = xs[xb_]
            src = T["d_x"][tt * 128:(tt + 1) * 128, :]
            sch.op("sp", lambda sbuf=sbuf, src=src: nc.sync.dma_start(out=sbuf[:], in_=src), (), [("xs", xb_)],
                   dma=("xs", xb_))
            for half in range(2):
                b = psalloc()
                for i in range(4):
                    kc = half * 4 + i
                    MM(ps[b][:, i * 128:(i + 1) * 128], sbuf[:, kc * 128:(kc + 1) * 128], ident[:], True, True,
                       [("xs", xb_), ("ident",)], [("ps", b)])
                eng = "act"
                xkeys = [("xT", kc, tt) for kc in range(half * 4, half * 4 + 4)]
                COPY(eng, xT[:, half * 4:half * 4 + 4, tt * 128:(tt + 1) * 128],
                     ps[b][:, :].rearrange("p (a b) -> p a b", a=4), [("ps", b)], xkeys)

                sq, sqk = wbf()
                ACT(sq[:, :].rearrange("p (a b) -> p a b", a=4), xT[:, half * 4:half * 4 + 4, tt * 128:(tt + 1) * 128],
                    AF.Square, xkeys, [sqk])

                def fn(tt=tt, half=half, g=g, sq=sq, sqk=sqk):
                    tl = tt % 4
                    for i in range(4):
                        MM(ps[4 + g][:, tl * 128:(tl + 1) * 128], onesD, sq[:, i * 128:(i + 1) * 128],
                           half == 0 and i == 0, half == 1 and i == 3, [sqk] + CK, [("ps", 4 + g)])
                defer(fn, 3)
                tick()
            if tt % 4 == 3:
                flush()
                norm_group(g, 0, True)

    def rms_stats(g, pre=False):
        if pre:
            b = 4 + g
        else:
            b = psalloc()
            for kc in range(8):
                sq, sqk = wbf()
                ACT(sq[:], xT[:, kc, gsl(g)], AF.Square, xk([kc], g), [sqk])
                MM(ps[b][:], onesD, sq[:], kc == 0, kc == 7, [sqk] + CK, [("ps", b)])
        rt, rk = w32()
        ACT(rt[:], ps[b][:], AF.Sqrt, [("ps", b), ("epsb",)], [rk], bias=T["epsb"][:, 0:1])
        sch.op("dve", lambda: nc.vector.reciprocal(out=rt[:], in_=rt[:]), [rk], [rk])
        return rt, rk

    def norm_group(g, gcol, pre):
        rt, rk = rms_stats(g, pre)
        for kc in range(8):
            STT("dve", hT[:, kc, gsl(g)], xT[:, kc, gsl(g)], vecs[:, gcol + kc:gcol + kc + 1], rt[:],
                ALU.mult, ALU.mult, xk([kc], g) + [rk, ("vecs",)], [("hT", kc, g)])

    def ffn(w13, w2, next_gcol):
        aT = T["aT"]
        sch.barrier(["aT"])
        st["psmod"] = 6
        w13v = w13.rearrange("(kc p) c -> p kc c", p=128)
        w2v = w2.rearrange("(fc p) c -> p fc c", p=128)
        for half in range(2):
            for fi in range(NFH):
                f = half * NFH + fi
                wa, wak = ws.next(w13v[:, :, f * 128:(f + 1) * 128], 8)
                wb, wbk = ws.next(w13v[:, :, DFF + f * 128:DFF + (f + 1) * 128], 8)
                for g in range(NG):
                    pa = psalloc()
                    pb = psalloc()
                    for kc in range(8):
                        MM(ps[pa][:], wa[:, kc, :], hT[:, kc, gsl(g)], kc == 0, kc == 7, [wak, ("hT", kc, g)], [("ps", pa)])
                    for kc in range(8):
                        MM(ps[pb][:], wb[:, kc, :], hT[:, kc, gsl(g)], kc == 0, kc == 7, [wbk, ("hT", kc, g)], [("ps", pb)])
                    s, sk = w32()
                    ACT(s[:], ps[pa][:], AF.Silu, [("ps", pa)], [sk])
                    TT("dve", aT[:, fi, gsl(g)], s[:], ps[pb][:], ALU.mult, [sk, ("ps", pb)], [("aT", fi, g)])
            if half == 1:
                st["psmod"] = 4
            for dc in range(8):
                w2a, w2ak = ws.next(w2v[:, half * NFH:half * NFH + 6, dc * 128:(dc + 1) * 128], 6)
                w2b, w2bk = ws.next(w2v[:, half * NFH + 6:(half + 1) * NFH, dc * 128:(dc + 1) * 128], NFH - 6)
                for g in range(NG):
                    po = psalloc()
                    for fi in range(NFH):
                        wt_, wk_ = (w2a[:, fi, :], w2ak) if fi < 6 else (w2b[:, fi - 6, :], w2bk)
                        MM(ps[po][:], wt_, aT[:, fi, gsl(g)], fi == 0, fi == NFH - 1,
                           [wk_, ("aT", fi, g)], [("ps", po)])
                    STT("dve", xT[:, dc, gsl(g)], ps[po][:], 0.5, xT[:, dc, gsl(g)], ALU.mult, ALU.add,
                        [("ps", po)] + xk([dc], g), xk([dc], g))
                    if half == 1:
                        stats_after_update(dc, g, next_gcol)
                    tick()
            if half == 1:
                flush()

    def mixer(l, vb, next_gcol):
        qT, kp, vv, cy, oT, mg = T["qT"], T["kp"], T["v"], T["cy"], T["oT"], T["mg"]
        ub = T["ub"]
        sch.barrier(["qT", "kp", "v", "cy", "oT"])
        gcol = vb + 8
        bgc = vb + 24
        cwc = vb + 40
        st["psmod"] = 8
        winv = T["d_w_in"][l].rearrange("(kc p) c -> p kc c", p=128)
        for buf in range(2):
            for (lo, hi, hh_) in ((64, 128, 0), (0, 64, 1)):
                sch.op("act", lambda o=kp[buf][lo:hi, hh_, :], i=hT[lo:hi, 0, :]: nc.scalar.mul(out=o, in_=i, mul=0.0),
                       [("hT", 0, g) for g in range(NG)], [("kp", buf, hh_, g, 1 - hh_) for g in range(NG)])

        ubc = 0
        for j in range(4):
            wcb, wcbk = ws.next(winv[:, :, j * 128:(j + 1) * 128], 8)
            wcc, wcck = ws.next(winv[:, :, 512 + j * 128:512 + (j + 1) * 128], 8)
            wcx, wcxk = ws.next(winv[:, :, 1024 + j * 128:1024 + (j + 1) * 128], 8)
            prev = None
            for g in range(NG):
                pcb, pcc, pcx = psalloc(), psalloc(), psalloc()
                for (pp, wt, wk_) in ((pcb, wcb, wcbk), (pcc, wcc, wcck), (pcx, wcx, wcxk)):
                    for kc in range(8):
                        MM(ps[pp][:], wt[:, kc, :], hT[:, kc, gsl(g)], kc == 0, kc == 7, [wk_, ("hT", kc, g)], [("ps", pp)])
                cxs, cxk = w32()
                COPY("act", cxs[:], ps[pcx][:], [("ps", pcx)], [cxk])
                u = ub[ubc % 2]
                uk = ("ub", ubc % 2)
                ukc = ("ubc", ubc % 2)
                ubc += 1
                if prev is None:
                    sch.op("act", lambda o=u[:, 0:2], i=vecs[:, 0:2]: nc.scalar.mul(out=o, in_=i, mul=0.0),
                           [("vecs",)], [ukc])
                else:
                    COPY("act", u[:, 0:2], prev[0][:, 512:514], [prev[1]], [ukc])
                TT("dve", u[:, 2:514], ps[pcc][:], cxs[:], ALU.mult, [("ps", pcc), cxk], [uk])
                prev = (u, uk)
                y, yk = w32()
                c0 = cwc + 0 * 4 + j
                c1 = cwc + 1 * 4 + j
                c2 = cwc + 2 * 4 + j
                TS1("dve", y[:], u[:, 0:512], vecs[:, c0:c0 + 1], ALU.mult, [uk, ukc, ("vecs",)], [yk])
                STT("dve", y[:], u[:, 1:513], vecs[:, c1:c1 + 1], y[:], ALU.mult, ALU.add, [uk, ukc, yk, ("vecs",)], [yk])
                STT("dve", y[:], u[:, 2:514], vecs[:, c2:c2 + 1], y[:], ALU.mult, ALU.add, [uk, yk, ("vecs",)], [yk])
                TT("dve", cy[:, j, gsl(g)], y[:], ps[pcb][:], ALU.mult, [yk, ("ps", pcb)], [("cy", j, g)])

        def a_qkv(hp, buf):
            wq, wqk = ws.next(winv[:, :, 1536 + hp * 128:1536 + (hp + 1) * 128], 8)
            wkk, wkkk = ws.next(winv[:, :, 2048 + hp * 128:2048 + (hp + 1) * 128], 8)
            wv, wvk = ws.next(winv[:, :, 2560 + hp * 128:2560 + (hp + 1) * 128], 8)
            for g in range(NG):
                pq, pk = psalloc(), psalloc()
                for kc in range(8):
                    MM(ps[pq][:], wq[:, kc, :], hT[:, kc, gsl(g)], kc == 0, kc == 7, [wqk, ("hT", kc, g)], [("ps", pq)])
                for kc in range(8):
                    MM(ps[pk][:], wkk[:, kc, :], hT[:, kc, gsl(g)], kc == 0, kc == 7, [wkkk, ("hT", kc, g)], [("ps", pk)])
                sch.op("act", lambda o=qT[buf][:, gsl(g)], i=ps[pq][:]: nc.scalar.mul(out=o, in_=i, mul=0.125), [("ps", pq)], [("qT", buf, g)])
                COPY("dve", kp[buf][0:64, 0, gsl(g)], ps[pk][0:64, :], [("ps", pk)], [("kp", buf, 0, g, 0)])
                COPY("dve", kp[buf][64:128, 1, gsl(g)], ps[pk][64:128, :], [("ps", pk)], [("kp", buf, 1, g, 1)])
            for t4 in range(NT // 4):
                pv = psalloc()
                for i in range(4):
                    tt = t4 * 4 + i
                    for kc in range(8):
                        MM(ps[pv][:, i * 128:(i + 1) * 128], hT[:, kc, tt * 128:(tt + 1) * 128], wv[:, kc, :],
                           kc == 0, kc == 7, [wvk, ("hT", kc, t4)], [("ps", pv)])
                COPY("dve", vv[buf][:, t4 * 4:t4 * 4 + 4, :], ps[pv][:, :].rearrange("p (a b) -> p a b", a=4),
                     [("ps", pv)], [("v", buf, t4)])

        rpool = T["rp"]

        def attn(hp, buf):
            blocks = []
            for qt in range(NG):
                nblk = 4 * qt + 4
                grp = dict(R=None)
                for bi, sb in enumerate(reversed(range(nblk))):
                    blocks.append(dict(qt=qt, sb=sb, bi=bi, nblk=nblk, grp=grp, idx=len(blocks)))

            def geom(B):
                qt, sb = B["qt"], B["sb"]
                r = sb - 4 * qt
                c0 = max(r, 0) * 128
                return r, c0, slice(c0, 512), slice(qt * 512 + c0, (qt + 1) * 512), slice(sb * 128, (sb + 1) * 128)

            def kkeys(B, hh):
                return [("kp", buf, hh, B["sb"] // 4, 0), ("kp", buf, hh, B["sb"] // 4, 1), ("qT", buf, B["qt"])]

            def S1a(B):
                r, c0, cs, qs, ksl = geom(B)
                for hh in range(2):
                    for j in range(NWARM):
                        MM(ps[hh][:, cs], zerosW, qT[buf][:, qs], j == 0, False, [("qT", buf, B["qt"])] + CK, [("ps", hh)])
                    MM(ps[hh][:, cs], kp[buf][:, hh, ksl], qT[buf][:, qs], NWARM == 0, r < 0, kkeys(B, hh), [("ps", hh)])
                    if r >= 0:
                        MM(ps[hh][:, c0:c0 + 128], identb, negmask[0][:, 0:128], False, True, CK, [("ps", hh)])
                e, ek = wide(ew, "ew")
                ACT(v3(e)[:, :, cs], v3(ppair[0])[:, :, cs], AF.Exp, [("ps", 0), ("ps", 1)], ek)
                B["e"], B["ek"] = e, ek

            def S1b(B):
                r, c0, cs, qs, ksl = geom(B)
                e, ek = B["e"], B["ek"]
                lt, lk = wide(bw, "bw")
                if c0 > 0:
                    MEMSET("pool", v3(lt)[:, :, 0:c0], 0.0, lk)
                ACT(v3(lt)[:, :, cs], v3(e)[:, :, cs], AF.Ln, ek + [("oneb",)], lk, bias=T["oneb"][:, 0:1])
                B["lt"], B["lk"] = lt, lk
                g = B["grp"]
                B["Rin"] = g["R"]
                if B["sb"] > 0:
                    if g["R"] is None:
                        g["R"] = (lt, lk)
                    else:
                        Rn, Rnk = wide(rpool, "rp")
                        TT("dve", Rn[:, :], g["R"][0][:, :], lt[:, :], ALU.add, g["R"][1] + lk, Rnk)
                        g["R"] = (Rn, Rnk)

            def S2a(B):
                r, c0, cs, qs, ksl = geom(B)
                pcp = 1 + (B["idx"] % 2)
                Rin = B["Rin"]
                for hh in range(2):
                    b = 2 * pcp + hh
                    MM(ps[b][:, cs], kp[buf][:, hh, ksl], qT[buf][:, qs], True, False, kkeys(B, hh), [("ps", b)])
                    if r >= 0:
                        MM(ps[b][:, c0:c0 + 128], identb, negmask[0][:, 0:128], False, False, CK, [("ps", b)])
                    MM(ps[b][:, cs], negTri, v3(B["lt"])[:, hh, cs], False, Rin is None, B["lk"] + CK, [("ps", b)])
                    if Rin is not None:
                        MM(ps[b][:, cs], negOnes, v3(Rin[0])[:, hh, cs], False, True, Rin[1] + CK, [("ps", b)])
                B["pcp"] = pcp

            def S2b(B):
                r, c0, cs, qs, ksl = geom(B)
                pcp = B["pcp"]
                a, ak = wide(bw, "bw")
                if B["bi"] == 0 and c0 > 0:
                    MEMSET("pool", v3(a)[:, :, 0:c0], 0.0, ak)
                ACT(v3(a)[:, :, cs], v3(ppair[pcp])[:, :, cs], AF.Exp, [("ps", 2 * pcp), ("ps", 2 * pcp + 1)], ak)
                B["a"], B["ak"] = a, ak

            def S3(B):
                r, c0, cs, qs, ksl = geom(B)
                qt, sb, bi, nblk = B["qt"], B["sb"], B["bi"], B["nblk"]
                if bi == 0:
                    cs = slice(0, 512)
                for hh in range(2):
                    MM(ps[6 + hh][:, cs], vv[buf][:, sb, :], v3(B["a"])[:, hh, cs], bi == 0, bi == nblk - 1,
                       B["ak"] + [("v", buf, sb // 4)], [("ps", 6 + hh)])
                if bi == nblk - 1:
                    for hh in range(2):
                        po = hh * 64
                        COPY("dve", oT[po:po + 64, hp, gsl(qt)], ps[6 + hh][po:po + 64, :], [("ps", 6 + hh)],
                             [("oT", hp, qt, hh)])

            n = len(blocks)
            for i in range(n + 3):
                if i < n:
                    S1a(blocks[i])
                if 0 <= i - 1 < n:
                    S2a(blocks[i - 1])
                if 0 <= i - 2 < n:
                    S2b(blocks[i - 2])
                if i < n:
                    S1b(blocks[i])
                if 0 <= i - 3 < n:
                    S3(blocks[i - 3])

        a_qkv(0, 0)
        for hp in range(4):
            if hp + 1 < 4:
                a_qkv(hp + 1, (hp + 1) % 2)
            attn(hp, hp % 2)

        sch.barrier(["mg"], old_names=("qT", "kp", "v"))
        woc = T["d_w_conv_o"][l].rearrange("(kc p) c -> p kc c", p=128)
        woa = T["d_w_attn_o"][l].rearrange("(kc p) c -> p kc c", p=128)
        for dc in range(8):
            wgc, wgck = ws.next(winv[:, :, 3072 + dc * 128:3072 + (dc + 1) * 128], 8)
            wga, wgak = ws.next(winv[:, :, 4096 + dc * 128:4096 + (dc + 1) * 128], 8)
            wco, wcok = ws.next(woc[:, :, dc * 128:(dc + 1) * 128], 4)
            wao, waok = ws.next(woa[:, :, dc * 128:(dc + 1) * 128], 4)
            for g in range(NG):
                pgc, pga, pcb_, pab = psalloc(), psalloc(), psalloc(), psalloc()
                for kc in range(8):
                    MM(ps[pgc][:], wgc[:, kc, :], hT[:, kc, gsl(g)], kc == 0, kc == 7, [wgck, ("hT", kc, g)], [("ps", pgc)])
                for kc in range(8):
                    MM(ps[pga][:], wga[:, kc, :], hT[:, kc, gsl(g)], kc == 0, kc == 7, [wgak, ("hT", kc, g)], [("ps", pga)])
                for j in range(4):
                    MM(ps[pcb_][:], wco[:, j, :], cy[:, j, gsl(g)], j == 0, j == 3, [wcok, ("cy", j, g)], [("ps", pcb_)])
                for j in range(4):
                    MM(ps[pab][:], wao[:, j, :], oT[:, j, gsl(g)], j == 0, j == 3,
                       [waok, ("oT", j, g, 0), ("oT", j, g, 1)], [("ps", pab)])
                gc, gck = w32()
                ga, gak = w32()
                ACT(gc[:], ps[pgc][:], AF.Sigmoid, [("ps", pgc), ("vecs",)], [gck], bias=vecs[:, bgc + dc:bgc + dc + 1])
                ACT(ga[:], ps[pga][:], AF.Sigmoid, [("ps", pga), ("vecs",)], [gak], bias=vecs[:, bgc + 8 + dc:bgc + 8 + dc + 1])
                TT("dve", gc[:], gc[:], ps[pcb_][:], ALU.mult, [gck, ("ps", pcb_)], [gck])
                TT("dve", ga[:], ga[:], ps[pab][:], ALU.mult, [gak, ("ps", pab)], [gak])
                TT("dve", mg[:, dc, gsl(g)], gc[:], ga[:], ALU.add, [gck, gak], [("mg", dc, g)])
        wov = T["d_w_out"][l].rearrange("(kc p) c -> p kc c", p=128)
        st["psmod"] = 4
        for dc in range(8):
            wo, wok = ws.next(wov[:, :, dc * 128:(dc + 1) * 128], 8)
            for g in range(NG):
                po = psalloc()
                for kc in range(8):
                    MM(ps[po][:], wo[:, kc, :], mg[:, kc, gsl(g)], kc == 0, kc == 7, [wok, ("mg", kc, g)], [("ps", po)])
                TT("dve", xT[:, dc, gsl(g)], ps[po][:], xT[:, dc, gsl(g)], ALU.add, [("ps", po)] + xk([dc], g), xk([dc], g))
                stats_after_update(dc, g, next_gcol)
                tick()
        flush()

    phase0()
    for l in range(depth):
        vb = l * 52
        ffn(T["d_ffn1_w13"][l], T["d_ffn1_w2"][l], vb + 8)
        mixer(l, vb, vb + 16)
        ffn(T["d_ffn2_w13"][l], T["d_ffn2_w2"][l], (l + 1) * 52 if l + 1 < depth else None)

    sch.barrier(["yT", "os"])
    st["psmod"] = 4
    yT, osb = T["yT"], T["os"]
    fcol = depth * 52
    def fin_norm(g):
        rt, rk = rms_stats(g, True)
        yb = yT[g % 2]
        for kc in range(8):
            STT("dve", yb[:, kc, :], xT[:, kc, gsl(g)], vecs[:, fcol + kc:fcol + kc + 1], rt[:], ALU.mult, ALU.mult,
                xk([kc], g) + [rk, ("vecs",)], [("yT", g % 2, kc)])

    fin_norm(0)
    for g in range(NG):
        if g + 1 < NG:
            fin_norm(g + 1)
        yb = yT[g % 2]
        for ti in range(4):
            tt = 4 * g + ti
            so = osb[tt % 2]
            for half in range(2):
                b = psalloc()
                for i in range(4):
                    kc = half * 4 + i
                    MM(ps[b][:, i * 128:(i + 1) * 128], yb[:, kc, ti * 128:(ti + 1) * 128], ident[:], True, True,
                       [("yT", g % 2, kc), ("ident",)], [("ps", b)])
                COPY("act", so[:, half * 512:(half + 1) * 512], ps[b][:], [("ps", b)], [("os", tt % 2, half)])
            dst = T["d_out"][tt * 128:(tt + 1) * 128, :]
            sch.op("sp", lambda so=so, dst=dst: nc.sync.dma_start(out=dst, in_=so[:]),
                   [("os", tt % 2, 0), ("os", tt % 2, 1)], [], dma=("os", tt % 2))


def build(S=2048, depth=2):
    nc = bass.Bass("TRN2", target_bir_lowering=False)
    T = {}

    def din(name, shape):
        return nc.dram_tensor(name, list(shape), F32, kind="ExternalInput").ap()

    T["d_x"] = din("x", [S, D])
    T["d_ffn1_w13"] = din("ffn1_w13", [depth, D, 2 * DFF])
    T["d_ffn1_w2"] = din("ffn1_w2", [depth, DFF, D])
    T["d_w_in"] = din("w_in", [depth, D, 5120])
    T["d_w_conv_o"] = din("w_conv_o", [depth, 512, D])
    T["d_w_attn_o"] = din("w_attn_o", [depth, 512, D])
    T["d_w_out"] = din("w_out", [depth, D, D])
    T["d_ffn2_w13"] = din("ffn2_w13", [depth, D, 2 * DFF])
    T["d_ffn2_w2"] = din("ffn2_w2", [depth, DFF, D])
    NV = depth * 52 + 8
    T["d_vecs"] = din("vecs", [128, NV])
    T["d_ident"] = din("ident", [128, 128])
    T["d_cbf"] = din("cbf", [128, 768])
    T["d_out"] = nc.dram_tensor("out", [S, D], F32, kind="ExternalOutput").ap()

    NT = S // 128
    with ExitStack() as stack:
        def sb(name, shape, dt):
            return stack.enter_context(nc.sbuf_tensor(name, list(shape), dt))

        T["xT"] = sb("xT", [128, 8, S], F32)
        T["hT"] = sb("hT", [128, 8, S], BF16)
        bigw = max(NFH * S * 2, 32 * S, 40960) // 4
        big = sb("big", [128, bigw], F32)
        bigb = big.bitcast(BF16)

        def bview(off_bytes, shape):
            n = int(np.prod(shape[1:]))
            o = off_bytes // 2
            ap = bigb[:, o:o + n]
            if len(shape) == 3:
                ap = ap.rearrange("p (a b) -> p a b", a=shape[1])
            return ap

        def fview(off_bytes, shape):
            n = int(np.prod(shape[1:]))
            o = off_bytes // 4
            ap = big[:, o:o + n]
            if len(shape) == 3:
                ap = ap.rearrange("p (a b) -> p a b", a=shape[1])
            return ap

        T["aT"] = bview(0, [128, NFH, S])
        per = 2 * S + 4 * S + 2 * S
        T["qT"] = [bview(b * per, [128, S]) for b in range(2)]
        T["kp"] = [bview(b * per + 2 * S, [128, 2, S]) for b in range(2)]
        T["v"] = [bview(b * per + 6 * S, [128, NT, 128]) for b in range(2)]
        T["mg"] = bview(0, [128, 8, S])
        T["cy"] = bview(16 * S, [128, 4, S])
        T["oT"] = bview(24 * S, [128, 4, S])
        T["xs"] = [fview(b * 4096, [128, 1024]) for b in range(6)]
        T["yT"] = [fview(b * 16384, [128, 8, 512]) for b in range(2)]
        T["os"] = [fview(32768 + b * 4096, [128, 1024]) for b in range(2)]

        T["ew"] = [sb(f"ew_{i}", [128, 1024], F32) for i in range(3)]
        T["bw"] = [sb(f"bw_{i}", [128, 1024], BF16) for i in range(5)]
        T["rp"] = [sb(f"rp_{i}", [128, 1024], BF16) for i in range(3)]
        T["ub"] = [sb(f"ub{i}", [128, 514], F32) for i in range(2)]
        slots = [sb(f"wslot{i}", [128, 8, 128], BF16) for i in range(NSLOT)]
        T["vecs"] = sb("vecs_sb", [128, NV], F32)
        T["ident"] = sb("ident_sb", [128, 128], F32)
        T["cbf"] = sb("cbf_sb", [128, 768], BF16)
        T["epsb"] = sb("epsb", [128, 1], F32)
        T["oneb"] = sb("oneb", [128, 1], F32)
        T["pp"] = [stack.enter_context(nc.psum_tensor(f"pp{i}", [128, 1024], F32)) for i in range(4)]
        T["ps"] = [T["pp"][i // 2][:, (i % 2) * 512:(i % 2 + 1) * 512] for i in range(8)]

        dry = Sched(nc, dry=True)
        rec = WStream(nc, dry, slots)
        emit_program(nc, T, dry, rec, S, depth)
        sch = Sched(nc)
        sch.op("dve", lambda: nc.vector.memset(T["epsb"][:], EPS), (), [("epsb",)])
        sch.op("dve", lambda: nc.vector.memset(T["oneb"][:], 1.0), (), [("oneb",)])
        ws = WStream(nc, sch, slots, order=rec.order)
        emit_program(nc, T, sch, ws, S, depth)
        n = sch.emit(stack)
    return nc, n


def host_consts():
    ident = np.eye(128, dtype=np.float32)
    j = np.arange(128)[:, None]
    s = np.arange(128)[None, :]
    negtri = np.where(j >= s, -1.0, 0.0).astype(np.float32)
    negones = -np.ones((128, 128), np.float32)
    onesd = np.full((128, 128), 1.0 / D, np.float32)
    i = np.arange(128)[:, None]
    c = np.arange(128)[None, :]
    mask = np.where(c <= i, NEG, 0.0).astype(np.float32)
    cbf = np.concatenate([negtri, negones, onesd, ident, mask, np.zeros((128, 128), np.float32)], axis=1)
    return ident, np.ascontiguousarray(cbf)


def pack_vecs(inp, depth):
    cols = []

    def pm(v):
        return np.asarray(v, np.float32).reshape(-1, 128).T

    for l in range(depth):
        cols.append(pm(inp["ffn1_norm"][l]))
        cols.append(pm(inp["mix_norm"][l]))
        cols.append(pm(inp["ffn2_norm"][l]))
        cols.append(pm(inp["b_gate"][l]))
        cw = np.asarray(inp["conv_w"][l], np.float32)
        cols.append(np.concatenate([pm(cw[k]) for k in range(3)], axis=1))
    cols.append(pm(inp["final_norm"]))
    return np.ascontiguousarray(np.concatenate(cols, axis=1))


_CACHE = {}


def kernel(**inputs):
    x = np.asarray(inputs["x"], np.float32)
    B, S, _ = x.shape
    depth = int(np.asarray(inputs["ffn1_w13"]).shape[0])
    key = (S, depth)
    if key not in _CACHE:
        _CACHE[key] = build(S, depth)[0]
    nc = _CACHE[key]
    ident, cbf = host_consts()
    vecs = pack_vecs(inputs, depth)
    shared = {
        "ffn1_w13": np.ascontiguousarray(inputs["ffn1_w13"], np.float32),
        "ffn1_w2": np.ascontiguousarray(inputs["ffn1_w2"], np.float32),
        "w_in": np.ascontiguousarray(inputs["w_in"], np.float32),
        "w_conv_o": np.ascontiguousarray(inputs["w_conv_o"], np.float32),
        "w_attn_o": np.ascontiguousarray(inputs["w_attn_o"], np.float32),
        "w_out": np.ascontiguousarray(inputs["w_out"], np.float32),
        "ffn2_w13": np.ascontiguousarray(inputs["ffn2_w13"], np.float32),
        "ffn2_w2": np.ascontiguousarray(inputs["ffn2_w2"], np.float32),
        "vecs": vecs, "ident": ident, "cbf": cbf,
    }
    in_maps = []
    for b in range(B):
        m = dict(shared)
        m["x"] = np.ascontiguousarray(x[b])
        in_maps.append(m)
    res = run_bass_kernel_spmd(nc, in_maps, core_ids=list(range(B)))
    return np.stack([np.asarray(r["out"], np.float32) for r in res.results], axis=0)
```

```python
import numpy as np
from contextlib import ExitStack
import concourse.bass as bass
import concourse.mybir as mybir
from concourse.bass_utils import run_bass_kernel_spmd

F32 = mybir.dt.float32
BF16 = mybir.dt.bfloat16
AF = mybir.ActivationFunctionType
ALU = mybir.AluOpType

D = 1024
DFF = 2816
NFH = 11
EPS = 1e-6
NSLOT = 6
NEG = -30000.0
NWARM = 1

BIG_NAMES = ("aT", "qT", "kp", "v", "cy", "oT", "mg", "xs", "yT", "os")


class Sched:
    def __init__(self, nc, dry=False):
        self.nc = nc
        self.dry = dry
        self.ops = []
        self.lastw = {}
        self.readers = {}
        self.alias_deps = {}
        self.ecount = {}

    def op(self, eng, fn, r=(), w=(), dma=None):
        if self.dry:
            return
        i = len(self.ops)
        deps = set()
        for k in r:
            p = self.lastw.get(k)
            if p is not None:
                deps.add(p)
        for k in w:
            p = self.lastw.get(k)
            if p is not None:
                deps.add(p)
            for q in self.readers.get(k, ()):
                deps.add(q)
            ad = self.alias_deps.get(k[0])
            if ad:
                deps |= ad
        eidx = self.ecount.get(eng, 0)
        self.ecount[eng] = eidx + 1
        best = {}
        keep = []
        for d in deps:
            P = self.ops[d]
            if P["dma"] is not None:
                keep.append(d)
            else:
                b = best.get(P["eng"])
                if b is None or d > b:
                    best[P["eng"]] = d
        keep.extend(best.values())
        self.ops.append(dict(eng=eng, fn=fn, dma=dma, deps=keep, eidx=eidx, sig=False, sigval=0))
        for k in r:
            lst = self.readers.setdefault(k, [])
            if dma is None:
                lst[:] = [q for q in lst if self.ops[q]["dma"] is not None or self.ops[q]["eng"] != eng]
            lst.append(i)
        for k in w:
            self.lastw[k] = i
            self.readers[k] = []

    def barrier(self, new_names, old_names=BIG_NAMES):
        if self.dry:
            return
        dset = set()
        for k, p in self.lastw.items():
            if k[0] in old_names:
                dset.add(p)
        for k, lst in self.readers.items():
            if k[0] in old_names:
                dset.update(lst)
        best = {}
        keep = set()
        for d in dset:
            P = self.ops[d]
            if P["dma"] is not None:
                keep.add(d)
            else:
                b = best.get(P["eng"])
                if b is None or d > b:
                    best[P["eng"]] = d
        keep.update(best.values())
        for n in new_names:
            self.alias_deps[n] = set(keep)

    def emit(self, stack, final_waits_engine="sp"):
        nc = self.nc
        engobj = {"pe": nc.tensor, "act": nc.scalar, "dve": nc.vector, "pool": nc.gpsimd, "sp": nc.sync}
        ops = self.ops
        for op in ops:
            waits = []
            for d in op["deps"]:
                P = ops[d]
                if P["dma"] is not None:
                    waits.append(d)
                    continue
                if P["eng"] == op["eng"] and op["dma"] is None:
                    if op["eng"] == "pe":
                        continue
                P["sig"] = True
                waits.append(d)
            op["waits"] = waits
        cnt = {}
        dcnt = {}
        for op in ops:
            if op["dma"] is not None:
                dcnt[op["dma"]] = dcnt.get(op["dma"], 0) + 16
                op["sigval"] = dcnt[op["dma"]]
            elif op["sig"]:
                cnt[op["eng"]] = cnt.get(op["eng"], 0) + 1
                op["sigval"] = cnt[op["eng"]]
        esem = {e: stack.enter_context(nc.semaphore("s_" + e)) for e in ("pe", "act", "dve", "pool")}
        dsem = {}
        for k in dcnt:
            dsem[k] = stack.enter_context(nc.semaphore("d_" + "_".join(str(x) for x in k)))
        waited = {e: {} for e in engobj}
        for op in ops:
            E = engobj[op["eng"]]
            wd = waited[op["eng"]]
            for d in sorted(op["waits"]):
                P = ops[d]
                if P["dma"] is not None:
                    key = ("d", P["dma"])
                    sem = dsem[P["dma"]]
                else:
                    key = ("e", P["eng"])
                    sem = esem[P["eng"]]
                if wd.get(key, 0) >= P["sigval"]:
                    continue
                E.wait_ge(sem, P["sigval"])
                wd[key] = P["sigval"]
            ins = op["fn"]()
            if op["dma"] is not None:
                ins.then_inc(dsem[op["dma"]], 16)
            elif op["sig"]:
                ins.then_inc(esem[op["eng"]], 1)
        E = engobj[final_waits_engine]
        for k, v in dcnt.items():
            if k[0] == "os":
                E.wait_ge(dsem[k], v)
        return len(ops)


class WStream:
    def __init__(self, nc, sch, slots, order=None):
        self.nc = nc
        self.sch = sch
        self.slots = slots
        self.record = order is None
        self.order = [] if order is None else order
        self.issued = 0
        self.cur = 0

    def _issue_to(self, n):
        nc = self.nc
        while self.issued < min(n, len(self.order)):
            i = self.issued
            ap, nk = self.order[i]
            s = i % NSLOT
            dst = self.slots[s][:, 0:nk, :]

            def fn(dst=dst, ap=ap):
                return nc.gpsimd.dma_start(out=dst, in_=ap)
            self.sch.op("pool", fn, r=(), w=[("w", s)], dma=("w", s))
            self.issued += 1

    def next(self, ap, nk):
        if self.record:
            self.order.append((ap, nk))
            i = len(self.order) - 1
        else:
            i = self.cur
            self.cur += 1
            self._issue_to(i + NSLOT - 3)
        s = i % NSLOT
        return self.slots[s], ("w", s)


def emit_program(nc, T, sch, ws, S, depth):
    NG = S // 512
    NT = S // 128
    xT, hT = T["xT"], T["hT"]
    ps = T["ps"]
    ppair = T["pp"]
    zerosW = T["cbf"][:, 640:768]
    vecs = T["vecs"]
    ident = T["ident"]
    cb = T["cbf"]
    negTri = cb[:, 0:128]
    negOnes = cb[:, 128:256]
    onesD = cb[:, 256:384]
    identb = cb[:, 384:512]
    negmask = [cb[:, 512:640]]
    CK = [("cbf",)]

    st = dict(ps=0, psacc=0, w32=0, wbf=0, alt=0, psmod=6)

    def psalloc():
        b = st["ps"] % st["psmod"]
        st["ps"] += 1
        return b

    pending = []

    def defer(fn, lag=2):
        pending.append([lag, fn])

    def tick():
        for it in pending:
            it[0] -= 1
        while pending and pending[0][0] <= 0:
            pending.pop(0)[1]()

    def flush():
        while pending:
            pending.pop(0)[1]()

    def stats_after_update(dc, g, next_gcol=None):
        sq, sqk = wbf()
        ACT(sq[:], xT[:, dc, gsl(g)], AF.Square, xk([dc], g), [sqk])

        def fn():
            MM(ps[4 + g][:], onesD, sq[:], dc == 0, dc == 7, [sqk] + CK, [("ps", 4 + g)])
            if dc == 7 and next_gcol is not None:
                norm_group(g, next_gcol, True)
        defer(fn)

    def psacc():
        b = 6 + st["psacc"] % 2
        st["psacc"] += 1
        return b

    ew, bw = T["ew"], T["bw"]

    def w32():
        i = st["w32"] % (2 * len(ew))
        st["w32"] += 1
        return ew[i // 2][:, (i % 2) * 512:(i % 2 + 1) * 512], ("ew", i // 2, i % 2)

    def wbf():
        i = st["wbf"] % (2 * len(bw))
        st["wbf"] += 1
        return bw[i // 2][:, (i % 2) * 512:(i % 2 + 1) * 512], ("bw", i // 2, i % 2)

    def wide(pool, name):
        cnt = "c_" + name
        i = st.setdefault(cnt, 0) % len(pool)
        st[cnt] += 1
        return pool[i], [(name, i, 0), (name, i, 1)]

    def v3(t):
        return t[:, :].rearrange("p (h c) -> p h c", h=2)

    def MM(out, lhsT, rhs, start, stop, r, w):
        sch.op("pe", lambda: nc.tensor.matmul(out, lhsT, rhs, start=start, stop=stop), r, w)

    def ACT(out, in_, func, r, w, bias=None, scale=None):
        kw = {}
        if bias is not None:
            kw["bias"] = bias
        if scale is not None:
            kw["scale"] = scale
        sch.op("act", lambda: nc.scalar.activation(out=out, in_=in_, func=func, **kw), r, w)

    def VEC(eng):
        return nc.vector if eng == "dve" else nc.gpsimd

    def TT(eng, out, in0, in1, op, r, w):
        sch.op(eng, lambda: VEC(eng).tensor_tensor(out=out, in0=in0, in1=in1, op=op), r, w)

    def STT(eng, out, in0, scalar, in1, op0, op1, r, w):
        sch.op(eng, lambda: VEC(eng).scalar_tensor_tensor(out=out, in0=in0, scalar=scalar, in1=in1, op0=op0, op1=op1), r, w)

    def TS1(eng, out, in0, scalar, op, r, w):
        sch.op(eng, lambda: VEC(eng).tensor_scalar(out=out, in0=in0, scalar1=scalar, scalar2=None, op0=op), r, w)

    def COPY(eng, out, in_, r, w):
        if eng == "act":
            sch.op("act", lambda: nc.scalar.activation(out=out, in_=in_, func=AF.Copy), r, w)
        else:
            sch.op(eng, lambda: VEC(eng).tensor_copy(out=out, in_=in_), r, w)

    def MEMSET(eng, ap, val, w):
        sch.op(eng, lambda: VEC(eng).memset(ap, val), (), w)

    def xk(kcs, g):
        return [("xT", kc, tt) for kc in kcs for tt in range(4 * g, 4 * g + 4)]

    def gsl(g):
        return slice(g * 512, (g + 1) * 512)

    sch.op("sp", lambda: nc.sync.dma_start(out=vecs[:], in_=T["d_vecs"][:, :]), (), [("vecs",)], dma=("c", 0))
    sch.op("sp", lambda: nc.sync.dma_start(out=ident[:], in_=T["d_ident"][:, :]), (), [("ident",)], dma=("c", 1))
    sch.op("pool", lambda: nc.gpsimd.dma_start(out=cb[:], in_=T["d_cbf"][:, :]), (), CK, dma=("c", 2))

    def phase0():
        xs = T["xs"]
        st["psmod"] = 4
        for tt in range(NT):
            g = tt // 4
            xb_ = tt % len(xs)
            sbuf = xs[xb_]
            src = T["d_x"][tt * 128:(tt + 1) * 128, :]
            sch.op("sp", lambda sbuf=sbuf, src=src: nc.sync.dma_start(out=sbuf[:], in_=src), (), [("xs", xb_)],
                   dma=("xs", xb_))
            for half in range(2):
                b = psalloc()
                for i in range(4):
                    kc = half * 4 + i
                    MM(ps[b][:, i * 128:(i + 1) * 128], sbuf[:, kc * 128:(kc + 1) * 128], ident[:], True, True,
                       [("xs", xb_), ("ident",)], [("ps", b)])
                eng = "act"
                xkeys = [("xT", kc, tt) for kc in range(half * 4, half * 4 + 4)]
                COPY(eng, xT[:, half * 4:half * 4 + 4, tt * 128:(tt + 1) * 128],
                     ps[b][:, :].rearrange("p (a b) -> p a b", a=4), [("ps", b)], xkeys)

                sq, sqk = wbf()
                ACT(sq[:, :].rearrange("p (a b) -> p a b", a=4), xT[:, half * 4:half * 4 + 4, tt * 128:(tt + 1) * 128],
                    AF.Square, xkeys, [sqk])

                def fn(tt=tt, half=half, g=g, sq=sq, sqk=sqk):
                    tl = tt % 4
                    for i in range(4):
                        MM(ps[4 + g][:, tl * 128:(tl + 1) * 128], onesD, sq[:, i * 128:(i + 1) * 128],
                           half == 0 and i == 0, half == 1 and i == 3, [sqk] + CK, [("ps", 4 + g)])
                defer(fn, 3)
                tick()
            if tt % 4 == 3:
                flush()
                norm_group(g, 0, True)

    def rms_stats(g, pre=False):
        if pre:
            b = 4 + g
        else:
            b = psalloc()
            for kc in range(8):
                sq, sqk = wbf()
                ACT(sq[:], xT[:, kc, gsl(g)], AF.Square, xk([kc], g), [sqk])
                MM(ps[b][:], onesD, sq[:], kc == 0, kc == 7, [sqk] + CK, [("ps", b)])
        rt, rk = w32()
        ACT(rt[:], ps[b][:], AF.Sqrt, [("ps", b), ("epsb",)], [rk], bias=T["epsb"][:, 0:1])
        sch.op("dve", lambda: nc.vector.reciprocal(out=rt[:], in_=rt[:]), [rk], [rk])
        return rt, rk

    def norm_group(g, gcol, pre):
        rt, rk = rms_stats(g, pre)
        for kc in range(8):
            STT("dve", hT[:, kc, gsl(g)], xT[:, kc, gsl(g)], vecs[:, gcol + kc:gcol + kc + 1], rt[:],
                ALU.mult, ALU.mult, xk([kc], g) + [rk, ("vecs",)], [("hT", kc, g)])

    def ffn(w13, w2, next_gcol):
        aT = T["aT"]
        sch.barrier(["aT"])
        st["psmod"] = 6
        w13v = w13.rearrange("(kc p) c -> p kc c", p=128)
        w2v = w2.rearrange("(fc p) c -> p fc c", p=128)
        for half in range(2):
            for fi in range(NFH):
                f = half * NFH + fi
                wa, wak = ws.next(w13v[:, :, f * 128:(f + 1) * 128], 8)
                wb, wbk = ws.next(w13v[:, :, DFF + f * 128:DFF + (f + 1) * 128], 8)
                for g in range(NG):
                    pa = psalloc()
                    pb = psalloc()
                    for kc in range(8):
                        MM(ps[pa][:], wa[:, kc, :], hT[:, kc, gsl(g)], kc == 0, kc == 7, [wak, ("hT", kc, g)], [("ps", pa)])
                    for kc in range(8):
                        MM(ps[pb][:], wb[:, kc, :], hT[:, kc, gsl(g)], kc == 0, kc == 7, [wbk, ("hT", kc, g)], [("ps", pb)])
                    s, sk = w32()
                    ACT(s[:], ps[pa][:], AF.Silu, [("ps", pa)], [sk])
                    TT("dve", aT[:, fi, gsl(g)], s[:], ps[pb][:], ALU.mult, [sk, ("ps", pb)], [("aT", fi, g)])
            if half == 1:
                st["psmod"] = 4
            for dc in range(8):
                w2a, w2ak = ws.next(w2v[:, half * NFH:half * NFH + 6, dc * 128:(dc + 1) * 128], 6)
                w2b, w2bk = ws.next(w2v[:, half * NFH + 6:(half + 1) * NFH, dc * 128:(dc + 1) * 128], NFH - 6)
                for g in range(NG):
                    po = psalloc()
                    for fi in range(NFH):
                        wt_, wk_ = (w2a[:, fi, :], w2ak) if fi < 6 else (w2b[:, fi - 6, :], w2bk)
                        MM(ps[po][:], wt_, aT[:, fi, gsl(g)], fi == 0, fi == NFH - 1,
                           [wk_, ("aT", fi, g)], [("ps", po)])
                    STT("dve", xT[:, dc, gsl(g)], ps[po][:], 0.5, xT[:, dc, gsl(g)], ALU.mult, ALU.add,
                        [("ps", po)] + xk([dc], g), xk([dc], g))
                    if half == 1:
                        stats_after_update(dc, g, next_gcol)
                    tick()
            if half == 1:
                flush()

    def mixer(l, vb, next_gcol):
        qT, kp, vv, cy, oT, mg = T["qT"], T["kp"], T["v"], T["cy"], T["oT"], T["mg"]
        ub = T["ub"]
        sch.barrier(["qT", "kp", "v", "cy", "oT"])
        gcol = vb + 8
        bgc = vb + 24
        cwc = vb + 40
        st["psmod"] = 8
        winv = T["d_w_in"][l].rearrange("(kc p) c -> p kc c", p=128)
        for buf in range(2):
            MEMSET("pool", kp[buf][64:128, 0, :], 0.0, [("kp", buf, 0, g, 1) for g in range(NG)])
            MEMSET("pool", kp[buf][0:64, 1, :], 0.0, [("kp", buf, 1, g, 0) for g in range(NG)])

        ubc = 0
        for j in range(4):
            wcb, wcbk = ws.next(winv[:, :, j * 128:(j + 1) * 128], 8)
            wcc, wcck = ws.next(winv[:, :, 512 + j * 128:512 + (j + 1) * 128], 8)
            wcx, wcxk = ws.next(winv[:, :, 1024 + j * 128:1024 + (j + 1) * 128], 8)
            prev = None
            for g in range(NG):
                pcb, pcc, pcx = psalloc(), psalloc(), psalloc()
                for (pp, wt, wk_) in ((pcb, wcb, wcbk), (pcc, wcc, wcck), (pcx, wcx, wcxk)):
                    for kc in range(8):
                        MM(ps[pp][:], wt[:, kc, :], hT[:, kc, gsl(g)], kc == 0, kc == 7, [wk_, ("hT", kc, g)], [("ps", pp)])
                cxs, cxk = w32()
                COPY("act", cxs[:], ps[pcx][:], [("ps", pcx)], [cxk])
                u = ub[ubc % 2]
                uk = ("ub", ubc % 2)
                ukc = ("ubc", ubc % 2)
                ubc += 1
                if prev is None:
                    MEMSET("pool", u[:, 0:2], 0.0, [ukc])
                else:
                    COPY("pool", u[:, 0:2], prev[0][:, 512:514], [prev[1]], [ukc])
                TT("dve", u[:, 2:514], ps[pcc][:], cxs[:], ALU.mult, [("ps", pcc), cxk], [uk])
                prev = (u, uk)
                y, yk = w32()
                c0 = cwc + 0 * 4 + j
                c1 = cwc + 1 * 4 + j
                c2 = cwc + 2 * 4 + j
                TS1("dve", y[:], u[:, 0:512], vecs[:, c0:c0 + 1], ALU.mult, [uk, ukc, ("vecs",)], [yk])
                STT("dve", y[:], u[:, 1:513], vecs[:, c1:c1 + 1], y[:], ALU.mult, ALU.add, [uk, ukc, yk, ("vecs",)], [yk])
                STT("dve", y[:], u[:, 2:514], vecs[:, c2:c2 + 1], y[:], ALU.mult, ALU.add, [uk, yk, ("vecs",)], [yk])
                TT("dve", cy[:, j, gsl(g)], y[:], ps[pcb][:], ALU.mult, [yk, ("ps", pcb)], [("cy", j, g)])

        def a_qkv(hp, buf):
            wq, wqk = ws.next(winv[:, :, 1536 + hp * 128:1536 + (hp + 1) * 128], 8)
            wkk, wkkk = ws.next(winv[:, :, 2048 + hp * 128:2048 + (hp + 1) * 128], 8)
            wv, wvk = ws.next(winv[:, :, 2560 + hp * 128:2560 + (hp + 1) * 128], 8)
            for g in range(NG):
                pq, pk = psalloc(), psalloc()
                for kc in range(8):
                    MM(ps[pq][:], wq[:, kc, :], hT[:, kc, gsl(g)], kc == 0, kc == 7, [wqk, ("hT", kc, g)], [("ps", pq)])
                for kc in range(8):
                    MM(ps[pk][:], wkk[:, kc, :], hT[:, kc, gsl(g)], kc == 0, kc == 7, [wkkk, ("hT", kc, g)], [("ps", pk)])
                sch.op("act", lambda o=qT[buf][:, gsl(g)], i=ps[pq][:]: nc.scalar.mul(out=o, in_=i, mul=0.125), [("ps", pq)], [("qT", buf, g)])
                COPY("dve", kp[buf][0:64, 0, gsl(g)], ps[pk][0:64, :], [("ps", pk)], [("kp", buf, 0, g, 0)])
                COPY("dve", kp[buf][64:128, 1, gsl(g)], ps[pk][64:128, :], [("ps", pk)], [("kp", buf, 1, g, 1)])
            for t4 in range(NT // 4):
                pv = psalloc()
                for i in range(4):
                    tt = t4 * 4 + i
                    for kc in range(8):
                        MM(ps[pv][:, i * 128:(i + 1) * 128], hT[:, kc, tt * 128:(tt + 1) * 128], wv[:, kc, :],
                           kc == 0, kc == 7, [wvk, ("hT", kc, t4)], [("ps", pv)])
                COPY("dve", vv[buf][:, t4 * 4:t4 * 4 + 4, :], ps[pv][:, :].rearrange("p (a b) -> p a b", a=4),
                     [("ps", pv)], [("v", buf, t4)])

        rpool = T["rp"]

        def attn(hp, buf):
            blocks = []
            for qt in range(NG):
                nblk = 4 * qt + 4
                grp = dict(R=None)
                for bi, sb in enumerate(reversed(range(nblk))):
                    blocks.append(dict(qt=qt, sb=sb, bi=bi, nblk=nblk, grp=grp, idx=len(blocks)))

            def geom(B):
                qt, sb = B["qt"], B["sb"]
                r = sb - 4 * qt
                c0 = max(r, 0) * 128
                return r, c0, slice(c0, 512), slice(qt * 512 + c0, (qt + 1) * 512), slice(sb * 128, (sb + 1) * 128)

            def kkeys(B, hh):
                return [("kp", buf, hh, B["sb"] // 4, 0), ("kp", buf, hh, B["sb"] // 4, 1), ("qT", buf, B["qt"])]

            def S1a(B):
                r, c0, cs, qs, ksl = geom(B)
                for hh in range(2):
                    for j in range(NWARM):
                        MM(ps[hh][:, cs], zerosW, qT[buf][:, qs], j == 0, False, [("qT", buf, B["qt"])] + CK, [("ps", hh)])
                    MM(ps[hh][:, cs], kp[buf][:, hh, ksl], qT[buf][:, qs], NWARM == 0, r < 0, kkeys(B, hh), [("ps", hh)])
                    if r >= 0:
                        MM(ps[hh][:, c0:c0 + 128], identb, negmask[0][:, 0:128], False, True, CK, [("ps", hh)])
                lt, lk = wide(bw, "bw")
                if c0 > 0:
                    MEMSET("pool", v3(lt)[:, :, 0:c0], 0.0, lk)
                B["lt"], B["lk"] = lt, lk
                e, ek = wide(ew, "ew")
                ACT(v3(e)[:, :, cs], v3(ppair[0])[:, :, cs], AF.Exp, [("ps", 0), ("ps", 1)], ek)
                B["e"], B["ek"] = e, ek

            def S1b(B):
                r, c0, cs, qs, ksl = geom(B)
                e, ek = B["e"], B["ek"]
                lt, lk = B["lt"], B["lk"]
                ACT(v3(lt)[:, :, cs], v3(e)[:, :, cs], AF.Ln, ek + [("oneb",)], lk, bias=T["oneb"][:, 0:1])
                g = B["grp"]
                B["Rin"] = g["R"]
                if B["sb"] > 0:
                    if g["R"] is None:
                        g["R"] = (lt, lk)
                    else:
                        Rn, Rnk = wide(rpool, "rp")
                        TT("dve", Rn[:, :], g["R"][0][:, :], lt[:, :], ALU.add, g["R"][1] + lk, Rnk)
                        g["R"] = (Rn, Rnk)

            def S2a(B):
                r, c0, cs, qs, ksl = geom(B)
                pcp = 1 + (B["idx"] % 2)
                Rin = B["Rin"]
                for hh in range(2):
                    b = 2 * pcp + hh
                    MM(ps[b][:, cs], kp[buf][:, hh, ksl], qT[buf][:, qs], True, False, kkeys(B, hh), [("ps", b)])
                    if r >= 0:
                        MM(ps[b][:, c0:c0 + 128], identb, negmask[0][:, 0:128], False, False, CK, [("ps", b)])
                    MM(ps[b][:, cs], negTri, v3(B["lt"])[:, hh, cs], False, Rin is None, B["lk"] + CK, [("ps", b)])
                    if Rin is not None:
                        MM(ps[b][:, cs], negOnes, v3(Rin[0])[:, hh, cs], False, True, Rin[1] + CK, [("ps", b)])
                B["pcp"] = pcp

            def S2b(B):
                r, c0, cs, qs, ksl = geom(B)
                pcp = B["pcp"]
                a, ak = wide(bw, "bw")
                if B["bi"] == 0 and c0 > 0:
                    MEMSET("pool", v3(a)[:, :, 0:c0], 0.0, ak)
                ACT(v3(a)[:, :, cs], v3(ppair[pcp])[:, :, cs], AF.Exp, [("ps", 2 * pcp), ("ps", 2 * pcp + 1)], ak)
                B["a"], B["ak"] = a, ak

            def S3(B):
                r, c0, cs, qs, ksl = geom(B)
                qt, sb, bi, nblk = B["qt"], B["sb"], B["bi"], B["nblk"]
                if bi == 0:
                    cs = slice(0, 512)
                for hh in range(2):
                    MM(ps[6 + hh][:, cs], vv[buf][:, sb, :], v3(B["a"])[:, hh, cs], bi == 0, bi == nblk - 1,
                       B["ak"] + [("v", buf, sb // 4)], [("ps", 6 + hh)])
                if bi == nblk - 1:
                    for hh in range(2):
                        po = hh * 64
                        COPY("dve", oT[po:po + 64, hp, gsl(qt)], ps[6 + hh][po:po + 64, :], [("ps", 6 + hh)],
                             [("oT", hp, qt, hh)])

            n = len(blocks)
            for i in range(n + 3):
                if i < n:
                    S1a(blocks[i])
                if 0 <= i - 1 < n:
                    S2a(blocks[i - 1])
                if 0 <= i - 2 < n:
                    S2b(blocks[i - 2])
                if i < n:
                    S1b(blocks[i])
                if 0 <= i - 3 < n:
                    S3(blocks[i - 3])

        a_qkv(0, 0)
        for hp in range(4):
            if hp + 1 < 4:
                a_qkv(hp + 1, (hp + 1) % 2)
            attn(hp, hp % 2)

        sch.barrier(["mg"], old_names=("qT", "kp", "v"))
        woc = T["d_w_conv_o"][l].rearrange("(kc p) c -> p kc c", p=128)
        woa = T["d_w_attn_o"][l].rearrange("(kc p) c -> p kc c", p=128)
        for dc in range(8):
            wgc, wgck = ws.next(winv[:, :, 3072 + dc * 128:3072 + (dc + 1) * 128], 8)
            wga, wgak = ws.next(winv[:, :, 4096 + dc * 128:4096 + (dc + 1) * 128], 8)
            wco, wcok = ws.next(woc[:, :, dc * 128:(dc + 1) * 128], 4)
            wao, waok = ws.next(woa[:, :, dc * 128:(dc + 1) * 128], 4)
            for g in range(NG):
                pgc, pga, pcb_, pab = psalloc(), psalloc(), psalloc(), psalloc()
                for kc in range(8):
                    MM(ps[pgc][:], wgc[:, kc, :], hT[:, kc, gsl(g)], kc == 0, kc == 7, [wgck, ("hT", kc, g)], [("ps", pgc)])
                for kc in range(8):
                    MM(ps[pga][:], wga[:, kc, :], hT[:, kc, gsl(g)], kc == 0, kc == 7, [wgak, ("hT", kc, g)], [("ps", pga)])
                for j in range(4):
                    MM(ps[pcb_][:], wco[:, j, :], cy[:, j, gsl(g)], j == 0, j == 3, [wcok, ("cy", j, g)], [("ps", pcb_)])
                for j in range(4):
                    MM(ps[pab][:], wao[:, j, :], oT[:, j, gsl(g)], j == 0, j == 3,
                       [waok, ("oT", j, g, 0), ("oT", j, g, 1)], [("ps", pab)])
                gc, gck = w32()
                ga, gak = w32()
                ACT(gc[:], ps[pgc][:], AF.Sigmoid, [("ps", pgc), ("vecs",)], [gck], bias=vecs[:, bgc + dc:bgc + dc + 1])
                ACT(ga[:], ps[pga][:], AF.Sigmoid, [("ps", pga), ("vecs",)], [gak], bias=vecs[:, bgc + 8 + dc:bgc + 8 + dc + 1])
                TT("dve", gc[:], gc[:], ps[pcb_][:], ALU.mult, [gck, ("ps", pcb_)], [gck])
                TT("dve", ga[:], ga[:], ps[pab][:], ALU.mult, [gak, ("ps", pab)], [gak])
                TT("dve", mg[:, dc, gsl(g)], gc[:], ga[:], ALU.add, [gck, gak], [("mg", dc, g)])
        wov = T["d_w_out"][l].rearrange("(kc p) c -> p kc c", p=128)
        st["psmod"] = 4
        for dc in range(8):
            wo, wok = ws.next(wov[:, :, dc * 128:(dc + 1) * 128], 8)
            for g in range(NG):
                po = psalloc()
                for kc in range(8):
                    MM(ps[po][:], wo[:, kc, :], mg[:, kc, gsl(g)], kc == 0, kc == 7, [wok, ("mg", kc, g)], [("ps", po)])
                TT("dve", xT[:, dc, gsl(g)], ps[po][:], xT[:, dc, gsl(g)], ALU.add, [("ps", po)] + xk([dc], g), xk([dc], g))
                stats_after_update(dc, g, next_gcol)
                tick()
        flush()

    phase0()
    for l in range(depth):
        vb = l * 52
        ffn(T["d_ffn1_w13"][l], T["d_ffn1_w2"][l], vb + 8)
        mixer(l, vb, vb + 16)
        ffn(T["d_ffn2_w13"][l], T["d_ffn2_w2"][l], (l + 1) * 52 if l + 1 < depth else None)

    sch.barrier(["yT", "os"])
    st["psmod"] = 4
    yT, osb = T["yT"], T["os"]
    fcol = depth * 52
    def fin_norm(g):
        rt, rk = rms_stats(g, True)
        yb = yT[g % 2]
        for kc in range(8):
            STT("dve", yb[:, kc, :], xT[:, kc, gsl(g)], vecs[:, fcol + kc:fcol + kc + 1], rt[:], ALU.mult, ALU.mult,
                xk([kc], g) + [rk, ("vecs",)], [("yT", g % 2, kc)])

    fin_norm(0)
    for g in range(NG):
        if g + 1 < NG:
            fin_norm(g + 1)
        yb = yT[g % 2]
        for ti in range(4):
            tt = 4 * g + ti
            so = osb[tt % 2]
            for half in range(2):
                b = psalloc()
                for i in range(4):
                    kc = half * 4 + i
                    MM(ps[b][:, i * 128:(i + 1) * 128], yb[:, kc, ti * 128:(ti + 1) * 128], ident[:], True, True,
                       [("yT", g % 2, kc), ("ident",)], [("ps", b)])
                COPY("act", so[:, half * 512:(half + 1) * 512], ps[b][:], [("ps", b)], [("os", tt % 2, half)])
            dst = T["d_out"][tt * 128:(tt + 1) * 128, :]
            sch.op("sp", lambda so=so, dst=dst: nc.sync.dma_start(out=dst, in_=so[:]),
                   [("os", tt % 2, 0), ("os", tt % 2, 1)], [], dma=("os", tt % 2))


def build(S=2048, depth=2):
    nc = bass.Bass("TRN2", target_bir_lowering=False)
    T = {}

    def din(name, shape):
        return nc.dram_tensor(name, list(shape), F32, kind="ExternalInput").ap()

    T["d_x"] = din("x", [S, D])
    T["d_ffn1_w13"] = din("ffn1_w13", [depth, D, 2 * DFF])
    T["d_ffn1_w2"] = din("ffn1_w2", [depth, DFF, D])
    T["d_w_in"] = din("w_in", [depth, D, 5120])
    T["d_w_conv_o"] = din("w_conv_o", [depth, 512, D])
    T["d_w_attn_o"] = din("w_attn_o", [depth, 512, D])
    T["d_w_out"] = din("w_out", [depth, D, D])
    T["d_ffn2_w13"] = din("ffn2_w13", [depth, D, 2 * DFF])
    T["d_ffn2_w2"] = din("ffn2_w2", [depth, DFF, D])
    NV = depth * 52 + 8
    T["d_vecs"] = din("vecs", [128, NV])
    T["d_ident"] = din("ident", [128, 128])
    T["d_cbf"] = din("cbf", [128, 768])
    T["d_out"] = nc.dram_tensor("out", [S, D], F32, kind="ExternalOutput").ap()

    NT = S // 128
    with ExitStack() as stack:
        def sb(name, shape, dt):
            return stack.enter_context(nc.sbuf_tensor(name, list(shape), dt))

        T["xT"] = sb("xT", [128, 8, S], F32)
        T["hT"] = sb("hT", [128, 8, S], BF16)
        bigw = max(NFH * S * 2, 32 * S, 40960) // 4
        big = sb("big", [128, bigw], F32)
        bigb = big.bitcast(BF16)

        def bview(off_bytes, shape):
            n = int(np.prod(shape[1:]))
            o = off_bytes // 2
            ap = bigb[:, o:o + n]
            if len(shape) == 3:
                ap = ap.rearrange("p (a b) -> p a b", a=shape[1])
            return ap

        def fview(off_bytes, shape):
            n = int(np.prod(shape[1:]))
            o = off_bytes // 4
            ap = big[:, o:o + n]
            if len(shape) == 3:
                ap = ap.rearrange("p (a b) -> p a b", a=shape[1])
            return ap

        T["aT"] = bview(0, [128, NFH, S])
        per = 2 * S + 4 * S + 2 * S
        T["qT"] = [bview(b * per, [128, S]) for b in range(2)]
        T["kp"] = [bview(b * per + 2 * S, [128, 2, S]) for b in range(2)]
        T["v"] = [bview(b * per + 6 * S, [128, NT, 128]) for b in range(2)]
        T["mg"] = bview(0, [128, 8, S])
        T["cy"] = bview(16 * S, [128, 4, S])
        T["oT"] = bview(24 * S, [128, 4, S])
        T["xs"] = [fview(b * 4096, [128, 1024]) for b in range(6)]
        T["yT"] = [fview(b * 16384, [128, 8, 512]) for b in range(2)]
        T["os"] = [fview(32768 + b * 4096, [128, 1024]) for b in range(2)]

        T["ew"] = [sb(f"ew_{i}", [128, 1024], F32) for i in range(3)]
        T["bw"] = [sb(f"bw_{i}", [128, 1024], BF16) for i in range(5)]
        T["rp"] = [sb(f"rp_{i}", [128, 1024], BF16) for i in range(3)]
        T["ub"] = [sb(f"ub{i}", [128, 514], F32) for i in range(2)]
        slots = [sb(f"wslot{i}", [128, 8, 128], BF16) for i in range(NSLOT)]
        T["vecs"] = sb("vecs_sb", [128, NV], F32)
        T["ident"] = sb("ident_sb", [128, 128], F32)
        T["cbf"] = sb("cbf_sb", [128, 768], BF16)
        T["epsb"] = sb("epsb", [128, 1], F32)
        T["oneb"] = sb("oneb", [128, 1], F32)
        T["pp"] = [stack.enter_context(nc.psum_tensor(f"pp{i}", [128, 1024], F32)) for i in range(4)]
        T["ps"] = [T["pp"][i // 2][:, (i % 2) * 512:(i % 2 + 1) * 512] for i in range(8)]

        dry = Sched(nc, dry=True)
        rec = WStream(nc, dry, slots)
        emit_program(nc, T, dry, rec, S, depth)
        sch = Sched(nc)
        sch.op("dve", lambda: nc.vector.memset(T["epsb"][:], EPS), (), [("epsb",)])
        sch.op("dve", lambda: nc.vector.memset(T["oneb"][:], 1.0), (), [("oneb",)])
        ws = WStream(nc, sch, slots, order=rec.order)
        emit_program(nc, T, sch, ws, S, depth)
        n = sch.emit(stack)
    return nc, n


def host_consts():
    ident = np.eye(128, dtype=np.float32)
    j = np.arange(128)[:, None]
    s = np.arange(128)[None, :]
    negtri = np.where(j >= s, -1.0, 0.0).astype(np.float32)
    negones = -np.ones((128, 128), np.float32)
    onesd = np.full((128, 128), 1.0 / D, np.float32)
    i = np.arange(128)[:, None]
    c = np.arange(128)[None, :]
    mask = np.where(c <= i, NEG, 0.0).astype(np.float32)
    cbf = np.concatenate([negtri, negones, onesd, ident, mask, np.zeros((128, 128), np.float32)], axis=1)
    return ident, np.ascontiguousarray(cbf)


def pack_vecs(inp, depth):
    cols = []

    def pm(v):
        return np.asarray(v, np.float32).reshape(-1, 128).T

    for l in range(depth):
        cols.append(pm(inp["ffn1_norm"][l]))
        cols.append(pm(inp["mix_norm"][l]))
        cols.append(pm(inp["ffn2_norm"][l]))
        cols.append(pm(inp["b_gate"][l]))
        cw = np.asarray(inp["conv_w"][l], np.float32)
        cols.append(np.concatenate([pm(cw[k]) for k in range(3)], axis=1))
    cols.append(pm(inp["final_norm"]))
    return np.ascontiguousarray(np.concatenate(cols, axis=1))


_CACHE = {}


def kernel(**inputs):
    x = np.asarray(inputs["x"], np.float32)
    B, S, _ = x.shape
    depth = int(np.asarray(inputs["ffn1_w13"]).shape[0])
    key = (S, depth)
    if key not in _CACHE:
        _CACHE[key] = build(S, depth)[0]
    nc = _CACHE[key]
    ident, cbf = host_consts()
    vecs = pack_vecs(inputs, depth)
    shared = {
        "ffn1_w13": np.ascontiguousarray(inputs["ffn1_w13"], np.float32),
        "ffn1_w2": np.ascontiguousarray(inputs["ffn1_w2"], np.float32),
        "w_in": np.ascontiguousarray(inputs["w_in"], np.float32),
        "w_conv_o": np.ascontiguousarray(inputs["w_conv_o"], np.float32),
        "w_attn_o": np.ascontiguousarray(inputs["w_attn_o"], np.float32),
        "w_out": np.ascontiguousarray(inputs["w_out"], np.float32),
        "ffn2_w13": np.ascontiguousarray(inputs["ffn2_w13"], np.float32),
        "ffn2_w2": np.ascontiguousarray(inputs["ffn2_w2"], np.float32),
        "vecs": vecs, "ident": ident, "cbf": cbf,
    }
    in_maps = []
    for b in range(B):
        m = dict(shared)
        m["x"] = np.ascontiguousarray(x[b])
        in_maps.append(m)
    res = run_bass_kernel_spmd(nc, in_maps, core_ids=list(range(B)))
    return np.stack([np.asarray(r["out"], np.float32) for r in res.results], axis=0)
```

```python
import numpy as np
from contextlib import ExitStack
import concourse.bass as bass
import concourse.mybir as mybir
from concourse.bass_utils import run_bass_kernel_spmd

F32 = mybir.dt.float32
BF16 = mybir.dt.bfloat16
AF = mybir.ActivationFunctionType
ALU = mybir.AluOpType

D = 1024
DFF = 2816
NFH = 11
EPS = 1e-6
NSLOT = 6
NEG = -30000.0
NWARM = 1

BIG_NAMES = ("aT", "qT", "kp", "v", "cy", "oT", "mg", "xs", "yT", "os")


class Sched:
    def __init__(self, nc, dry=False):
        self.nc = nc
        self.dry = dry
        self.ops = []
        self.lastw = {}
        self.readers = {}
        self.alias_deps = {}
        self.ecount = {}

    def op(self, eng, fn, r=(), w=(), dma=None):
        if self.dry:
            return
        i = len(self.ops)
        deps = set()
        for k in r:
            p = self.lastw.get(k)
            if p is not None:
                deps.add(p)
        for k in w:
            p = self.lastw.get(k)
            if p is not None:
                deps.add(p)
            for q in self.readers.get(k, ()):
                deps.add(q)
            ad = self.alias_deps.get(k[0])
            if ad:
                deps |= ad
        eidx = self.ecount.get(eng, 0)
        self.ecount[eng] = eidx + 1
        best = {}
        keep = []
        for d in deps:
            P = self.ops[d]
            if P["dma"] is not None:
                keep.append(d)
            else:
                b = best.get(P["eng"])
                if b is None or d > b:
                    best[P["eng"]] = d
        keep.extend(best.values())
        self.ops.append(dict(eng=eng, fn=fn, dma=dma, deps=keep, eidx=eidx, sig=False, sigval=0))
        for k in r:
            lst = self.readers.setdefault(k, [])
            if dma is None:
                lst[:] = [q for q in lst if self.ops[q]["dma"] is not None or self.ops[q]["eng"] != eng]
            lst.append(i)
        for k in w:
            self.lastw[k] = i
            self.readers[k] = []

    def barrier(self, new_names, old_names=BIG_NAMES):
        if self.dry:
            return
        dset = set()
        for k, p in self.lastw.items():
            if k[0] in old_names:
                dset.add(p)
        for k, lst in self.readers.items():
            if k[0] in old_names:
                dset.update(lst)
        best = {}
        keep = set()
        for d in dset:
            P = self.ops[d]
            if P["dma"] is not None:
                keep.add(d)
            else:
                b = best.get(P["eng"])
                if b is None or d > b:
                    best[P["eng"]] = d
        keep.update(best.values())
        for n in new_names:
            self.alias_deps[n] = set(keep)

    def emit(self, stack, final_waits_engine="sp"):
        nc = self.nc
        engobj = {"pe": nc.tensor, "act": nc.scalar, "dve": nc.vector, "pool": nc.gpsimd, "sp": nc.sync}
        ops = self.ops
        for op in ops:
            waits = []
            for d in op["deps"]:
                P = ops[d]
                if P["dma"] is not None:
                    waits.append(d)
                    continue
                if P["eng"] == op["eng"] and op["dma"] is None:
                    if op["eng"] == "pe":
                        continue
                P["sig"] = True
                waits.append(d)
            op["waits"] = waits
        cnt = {}
        dcnt = {}
        for op in ops:
            if op["dma"] is not None:
                dcnt[op["dma"]] = dcnt.get(op["dma"], 0) + 16
                op["sigval"] = dcnt[op["dma"]]
            elif op["sig"]:
                cnt[op["eng"]] = cnt.get(op["eng"], 0) + 1
                op["sigval"] = cnt[op["eng"]]
        esem = {e: stack.enter_context(nc.semaphore("s_" + e)) for e in ("pe", "act", "dve", "pool")}
        dsem = {}
        for k in dcnt:
            dsem[k] = stack.enter_context(nc.semaphore("d_" + "_".join(str(x) for x in k)))
        waited = {e: {} for e in engobj}
        for op in ops:
            E = engobj[op["eng"]]
            wd = waited[op["eng"]]
            for d in sorted(op["waits"]):
                P = ops[d]
                if P["dma"] is not None:
                    key = ("d", P["dma"])
                    sem = dsem[P["dma"]]
                else:
                    key = ("e", P["eng"])
                    sem = esem[P["eng"]]
                if wd.get(key, 0) >= P["sigval"]:
                    continue
                E.wait_ge(sem, P["sigval"])
                wd[key] = P["sigval"]
            ins = op["fn"]()
            if op["dma"] is not None:
                ins.then_inc(dsem[op["dma"]], 16)
            elif op["sig"]:
                ins.then_inc(esem[op["eng"]], 1)
        E = engobj[final_waits_engine]
        for k, v in dcnt.items():
            if k[0] == "os":
                E.wait_ge(dsem[k], v)
        return len(ops)


class WStream:
    def __init__(self, nc, sch, slots, order=None):
        self.nc = nc
        self.sch = sch
        self.slots = slots
        self.record = order is None
        self.order = [] if order is None else order
        self.issued = 0
        self.cur = 0

    def _issue_to(self, n):
        nc = self.nc
        while self.issued < min(n, len(self.order)):
            i = self.issued
            ap, nk = self.order[i]
            s = i % NSLOT
            dst = self.slots[s][:, 0:nk, :]

            def fn(dst=dst, ap=ap):
                return nc.gpsimd.dma_start(out=dst, in_=ap)
            self.sch.op("pool", fn, r=(), w=[("w", s)], dma=("w", s))
            self.issued += 1

    def next(self, ap, nk):
        if self.record:
            self.order.append((ap, nk))
            i = len(self.order) - 1
        else:
            i = self.cur
            self.cur += 1
            self._issue_to(i + NSLOT - 3)
        s = i % NSLOT
        return self.slots[s], ("w", s)


def emit_program(nc, T, sch, ws, S, depth):
    NG = S // 512
    NT = S // 128
    xT, hT = T["xT"], T["hT"]
    ps = T["ps"]
    ppair = T["pp"]
    zerosW = T["cbf"][:, 640:768]
    vecs = T["vecs"]
    ident = T["ident"]
    cb = T["cbf"]
    negTri = cb[:, 0:128]
    negOnes = cb[:, 128:256]
    onesD = cb[:, 256:384]
    identb = cb[:, 384:512]
    negmask = [cb[:, 512:640]]
    CK = [("cbf",)]

    st = dict(ps=0, psacc=0, w32=0, wbf=0, alt=0, psmod=6)

    def psalloc():
        b = st["ps"] % st["psmod"]
        st["ps"] += 1
        return b

    pending = []

    def defer(fn, lag=2):
        pending.append([lag, fn])

    def tick():
        for it in pending:
            it[0] -= 1
        while pending and pending[0][0] <= 0:
            pending.pop(0)[1]()

    def flush():
        while pending:
            pending.pop(0)[1]()

    def stats_after_update(dc, g, next_gcol=None):
        sq, sqk = wbf()
        ACT(sq[:], xT[:, dc, gsl(g)], AF.Square, xk([dc], g), [sqk])

        def fn():
            MM(ps[4 + g][:], onesD, sq[:], dc == 0, dc == 7, [sqk] + CK, [("ps", 4 + g)])
            if dc == 7 and next_gcol is not None:
                norm_group(g, next_gcol, True)
        defer(fn)

    def psacc():
        b = 6 + st["psacc"] % 2
        st["psacc"] += 1
        return b

    ew, bw = T["ew"], T["bw"]

    def w32():
        i = st["w32"] % (2 * len(ew))
        st["w32"] += 1
        return ew[i // 2][:, (i % 2) * 512:(i % 2 + 1) * 512], ("ew", i // 2, i % 2)

    def wbf():
        i = st["wbf"] % (2 * len(bw))
        st["wbf"] += 1
        return bw[i // 2][:, (i % 2) * 512:(i % 2 + 1) * 512], ("bw", i // 2, i % 2)

    def wide(pool, name):
        cnt = "c_" + name
        i = st.setdefault(cnt, 0) % len(pool)
        st[cnt] += 1
        return pool[i], [(name, i, 0), (name, i, 1)]

    def v3(t):
        return t[:, :].rearrange("p (h c) -> p h c", h=2)

    def MM(out, lhsT, rhs, start, stop, r, w):
        sch.op("pe", lambda: nc.tensor.matmul(out, lhsT, rhs, start=start, stop=stop), r, w)

    def ACT(out, in_, func, r, w, bias=None, scale=None):
        kw = {}
        if bias is not None:
            kw["bias"] = bias
        if scale is not None:
            kw["scale"] = scale
        sch.op("act", lambda: nc.scalar.activation(out=out, in_=in_, func=func, **kw), r, w)

    def VEC(eng):
        return nc.vector if eng == "dve" else nc.gpsimd

    def TT(eng, out, in0, in1, op, r, w):
        sch.op(eng, lambda: VEC(eng).tensor_tensor(out=out, in0=in0, in1=in1, op=op), r, w)

    def STT(eng, out, in0, scalar, in1, op0, op1, r, w):
        sch.op(eng, lambda: VEC(eng).scalar_tensor_tensor(out=out, in0=in0, scalar=scalar, in1=in1, op0=op0, op1=op1), r, w)

    def TS1(eng, out, in0, scalar, op, r, w):
        sch.op(eng, lambda: VEC(eng).tensor_scalar(out=out, in0=in0, scalar1=scalar, scalar2=None, op0=op), r, w)

    def COPY(eng, out, in_, r, w):
        if eng == "act":
            sch.op("act", lambda: nc.scalar.activation(out=out, in_=in_, func=AF.Copy), r, w)
        else:
            sch.op(eng, lambda: VEC(eng).tensor_copy(out=out, in_=in_), r, w)

    def MEMSET(eng, ap, val, w):
        sch.op(eng, lambda: VEC(eng).memset(ap, val), (), w)

    def xk(kcs, g):
        return [("xT", kc, tt) for kc in kcs for tt in range(4 * g, 4 * g + 4)]

    def gsl(g):
        return slice(g * 512, (g + 1) * 512)

    sch.op("sp", lambda: nc.sync.dma_start(out=vecs[:], in_=T["d_vecs"][:, :]), (), [("vecs",)], dma=("c", 0))
    sch.op("sp", lambda: nc.sync.dma_start(out=ident[:], in_=T["d_ident"][:, :]), (), [("ident",)], dma=("c", 1))
    sch.op("pool", lambda: nc.gpsimd.dma_start(out=cb[:], in_=T["d_cbf"][:, :]), (), CK, dma=("c", 2))

    def phase0():
        xs = T["xs"]
        st["psmod"] = 4
        for tt in range(NT):
            g = tt // 4
            xb_ = tt % len(xs)
            sbuf = xs[xb_]
            src = T["d_x"][tt * 128:(tt + 1) * 128, :]
            sch.op("sp", lambda sbuf=sbuf, src=src: nc.sync.dma_start(out=sbuf[:], in_=src), (), [("xs", xb_)],
                   dma=("xs", xb_))
            for half in range(2):
                b = psalloc()
                for i in range(4):
                    kc = half * 4 + i
                    MM(ps[b][:, i * 128:(i + 1) * 128], sbuf[:, kc * 128:(kc + 1) * 128], ident[:], True, True,
                       [("xs", xb_), ("ident",)], [("ps", b)])
                eng = "act"
                xkeys = [("xT", kc, tt) for kc in range(half * 4, half * 4 + 4)]
                COPY(eng, xT[:, half * 4:half * 4 + 4, tt * 128:(tt + 1) * 128],
                     ps[b][:, :].rearrange("p (a b) -> p a b", a=4), [("ps", b)], xkeys)

                sq, sqk = wbf()
                ACT(sq[:, :].rearrange("p (a b) -> p a b", a=4), xT[:, half * 4:half * 4 + 4, tt * 128:(tt + 1) * 128],
                    AF.Square, xkeys, [sqk])

                def fn(tt=tt, half=half, g=g, sq=sq, sqk=sqk):
                    tl = tt % 4
                    for i in range(4):
                        MM(ps[4 + g][:, tl * 128:(tl + 1) * 128], onesD, sq[:, i * 128:(i + 1) * 128],
                           half == 0 and i == 0, half == 1 and i == 3, [sqk] + CK, [("ps", 4 + g)])
                defer(fn, 3)
                tick()
            if tt % 4 == 3:
                flush()
                norm_group(g, 0, True)

    def rms_stats(g, pre=False):
        if pre:
            b = 4 + g
        else:
            b = psalloc()
            for kc in range(8):
                sq, sqk = wbf()
                ACT(sq[:], xT[:, kc, gsl(g)], AF.Square, xk([kc], g), [sqk])
                MM(ps[b][:], onesD, sq[:], kc == 0, kc == 7, [sqk] + CK, [("ps", b)])
        rt, rk = w32()
        ACT(rt[:], ps[b][:], AF.Sqrt, [("ps", b), ("epsb",)], [rk], bias=T["epsb"][:, 0:1])
        sch.op("dve", lambda: nc.vector.reciprocal(out=rt[:], in_=rt[:]), [rk], [rk])
        return rt, rk

    def norm_group(g, gcol, pre):
        rt, rk = rms_stats(g, pre)
        for kc in range(8):
            STT("dve", hT[:, kc, gsl(g)], xT[:, kc, gsl(g)], vecs[:, gcol + kc:gcol + kc + 1], rt[:],
                ALU.mult, ALU.mult, xk([kc], g) + [rk, ("vecs",)], [("hT", kc, g)])

    def ffn(w13, w2, next_gcol):
        aT = T["aT"]
        sch.barrier(["aT"])
        st["psmod"] = 6
        w13v = w13.rearrange("(kc p) c -> p kc c", p=128)
        w2v = w2.rearrange("(fc p) c -> p fc c", p=128)
        for half in range(2):
            for fi in range(NFH):
                f = half * NFH + fi
                wa, wak = ws.next(w13v[:, :, f * 128:(f + 1) * 128], 8)
                wb, wbk = ws.next(w13v[:, :, DFF + f * 128:DFF + (f + 1) * 128], 8)
                for g in range(NG):
                    pa = psalloc()
                    pb = psalloc()
                    for kc in range(8):
                        MM(ps[pa][:], wa[:, kc, :], hT[:, kc, gsl(g)], kc == 0, kc == 7, [wak, ("hT", kc, g)], [("ps", pa)])
                    for kc in range(8):
                        MM(ps[pb][:], wb[:, kc, :], hT[:, kc, gsl(g)], kc == 0, kc == 7, [wbk, ("hT", kc, g)], [("ps", pb)])
                    s, sk = w32()
                    ACT(s[:], ps[pa][:], AF.Silu, [("ps", pa)], [sk])
                    TT("dve", aT[:, fi, gsl(g)], s[:], ps[pb][:], ALU.mult, [sk, ("ps", pb)], [("aT", fi, g)])
            if half == 1:
                st["psmod"] = 4
            for dc in range(8):
                w2a, w2ak = ws.next(w2v[:, half * NFH:half * NFH + 6, dc * 128:(dc + 1) * 128], 6)
                w2b, w2bk = ws.next(w2v[:, half * NFH + 6:(half + 1) * NFH, dc * 128:(dc + 1) * 128], NFH - 6)
                for g in range(NG):
                    po = psalloc()
                    for fi in range(NFH):
                        wt_, wk_ = (w2a[:, fi, :], w2ak) if fi < 6 else (w2b[:, fi - 6, :], w2bk)
                        MM(ps[po][:], wt_, aT[:, fi, gsl(g)], fi == 0, fi == NFH - 1,
                           [wk_, ("aT", fi, g)], [("ps", po)])
                    STT("dve", xT[:, dc, gsl(g)], ps[po][:], 0.5, xT[:, dc, gsl(g)], ALU.mult, ALU.add,
                        [("ps", po)] + xk([dc], g), xk([dc], g))
                    if half == 1:
                        stats_after_update(dc, g, next_gcol)
                    tick()
            if half == 1:
                flush()

    def mixer(l, vb, next_gcol):
        qT, kp, vv, cy, oT, mg = T["qT"], T["kp"], T["v"], T["cy"], T["oT"], T["mg"]
        ub = T["ub"]
        sch.barrier(["qT", "kp", "v", "cy", "oT"])
        gcol = vb + 8
        bgc = vb + 24
        cwc = vb + 40
        st["psmod"] = 8
        winv = T["d_w_in"][l].rearrange("(kc p) c -> p kc c", p=128)
        for buf in range(2):
            MEMSET("pool", kp[buf][64:128, 0, :], 0.0, [("kp", buf, 0, g, 1) for g in range(NG)])
            MEMSET("pool", kp[buf][0:64, 1, :], 0.0, [("kp", buf, 1, g, 0) for g in range(NG)])

        ubc = 0
        for j in range(4):
            wcb, wcbk = ws.next(winv[:, :, j * 128:(j + 1) * 128], 8)
            wcc, wcck = ws.next(winv[:, :, 512 + j * 128:512 + (j + 1) * 128], 8)
            wcx, wcxk = ws.next(winv[:, :, 1024 + j * 128:1024 + (j + 1) * 128], 8)
            prev = None
            for g in range(NG):
                pcb, pcc, pcx = psalloc(), psalloc(), psalloc()
                for (pp, wt, wk_) in ((pcb, wcb, wcbk), (pcc, wcc, wcck), (pcx, wcx, wcxk)):
                    for kc in range(8):
                        MM(ps[pp][:], wt[:, kc, :], hT[:, kc, gsl(g)], kc == 0, kc == 7, [wk_, ("hT", kc, g)], [("ps", pp)])
                cxs, cxk = w32()
                COPY("act", cxs[:], ps[pcx][:], [("ps", pcx)], [cxk])
                u = ub[ubc % 2]
                uk = ("ub", ubc % 2)
                ukc = ("ubc", ubc % 2)
                ubc += 1
                if prev is None:
                    MEMSET("pool", u[:, 0:2], 0.0, [ukc])
                else:
                    COPY("pool", u[:, 0:2], prev[0][:, 512:514], [prev[1]], [ukc])
                TT("dve", u[:, 2:514], ps[pcc][:], cxs[:], ALU.mult, [("ps", pcc), cxk], [uk])
                prev = (u, uk)
                y, yk = w32()
                c0 = cwc + 0 * 4 + j
                c1 = cwc + 1 * 4 + j
                c2 = cwc + 2 * 4 + j
                TS1("dve", y[:], u[:, 0:512], vecs[:, c0:c0 + 1], ALU.mult, [uk, ukc, ("vecs",)], [yk])
                STT("dve", y[:], u[:, 1:513], vecs[:, c1:c1 + 1], y[:], ALU.mult, ALU.add, [uk, ukc, yk, ("vecs",)], [yk])
                STT("dve", y[:], u[:, 2:514], vecs[:, c2:c2 + 1], y[:], ALU.mult, ALU.add, [uk, yk, ("vecs",)], [yk])
                TT("dve", cy[:, j, gsl(g)], y[:], ps[pcb][:], ALU.mult, [yk, ("ps", pcb)], [("cy", j, g)])

        def a_qkv(hp, buf):
            wq, wqk = ws.next(winv[:, :, 1536 + hp * 128:1536 + (hp + 1) * 128], 8)
            wkk, wkkk = ws.next(winv[:, :, 2048 + hp * 128:2048 + (hp + 1) * 128], 8)
            wv, wvk = ws.next(winv[:, :, 2560 + hp * 128:2560 + (hp + 1) * 128], 8)
            for g in range(NG):
                pq, pk = psalloc(), psalloc()
                for kc in range(8):
                    MM(ps[pq][:], wq[:, kc, :], hT[:, kc, gsl(g)], kc == 0, kc == 7, [wqk, ("hT", kc, g)], [("ps", pq)])
                for kc in range(8):
                    MM(ps[pk][:], wkk[:, kc, :], hT[:, kc, gsl(g)], kc == 0, kc == 7, [wkkk, ("hT", kc, g)], [("ps", pk)])
                sch.op("act", lambda o=qT[buf][:, gsl(g)], i=ps[pq][:]: nc.scalar.mul(out=o, in_=i, mul=0.125), [("ps", pq)], [("qT", buf, g)])
                COPY("dve", kp[buf][0:64, 0, gsl(g)], ps[pk][0:64, :], [("ps", pk)], [("kp", buf, 0, g, 0)])
                COPY("dve", kp[buf][64:128, 1, gsl(g)], ps[pk][64:128, :], [("ps", pk)], [("kp", buf, 1, g, 1)])
            for t4 in range(NT // 4):
                pv = psalloc()
                for i in range(4):
                    tt = t4 * 4 + i
                    for kc in range(8):
                        MM(ps[pv][:, i * 128:(i + 1) * 128], hT[:, kc, tt * 128:(tt + 1) * 128], wv[:, kc, :],
                           kc == 0, kc == 7, [wvk, ("hT", kc, t4)], [("ps", pv)])
                COPY("dve", vv[buf][:, t4 * 4:t4 * 4 + 4, :], ps[pv][:, :].rearrange("p (a b) -> p a b", a=4),
                     [("ps", pv)], [("v", buf, t4)])

        rpool = T["rp"]

        def attn(hp, buf):
            blocks = []
            for qt in range(NG):
                nblk = 4 * qt + 4
                grp = dict(R=None)
                for bi, sb in enumerate(reversed(range(nblk))):
                    blocks.append(dict(qt=qt, sb=sb, bi=bi, nblk=nblk, grp=grp, idx=len(blocks)))

            def geom(B):
                qt, sb = B["qt"], B["sb"]
                r = sb - 4 * qt
                c0 = max(r, 0) * 128
                return r, c0, slice(c0, 512), slice(qt * 512 + c0, (qt + 1) * 512), slice(sb * 128, (sb + 1) * 128)

            def kkeys(B, hh):
                return [("kp", buf, hh, B["sb"] // 4, 0), ("kp", buf, hh, B["sb"] // 4, 1), ("qT", buf, B["qt"])]

            def S1a(B):
                r, c0, cs, qs, ksl = geom(B)
                for hh in range(2):
                    for j in range(NWARM):
                        MM(ps[hh][:, cs], zerosW, qT[buf][:, qs], j == 0, False, [("qT", buf, B["qt"])] + CK, [("ps", hh)])
                    MM(ps[hh][:, cs], kp[buf][:, hh, ksl], qT[buf][:, qs], NWARM == 0, r < 0, kkeys(B, hh), [("ps", hh)])
                    if r >= 0:
                        MM(ps[hh][:, c0:c0 + 128], identb, negmask[0][:, 0:128], False, True, CK, [("ps", hh)])
                e, ek = wide(ew, "ew")
                ACT(v3(e)[:, :, cs], v3(ppair[0])[:, :, cs], AF.Exp, [("ps", 0), ("ps", 1)], ek)
                B["e"], B["ek"] = e, ek

            def S1b(B):
                r, c0, cs, qs, ksl = geom(B)
                e, ek = B["e"], B["ek"]
                lt, lk = wide(bw, "bw")
                if c0 > 0:
                    MEMSET("dve", v3(lt)[:, :, 0:c0], 0.0, lk)
                ACT(v3(lt)[:, :, cs], v3(e)[:, :, cs], AF.Ln, ek + [("oneb",)], lk, bias=T["oneb"][:, 0:1])
                B["lt"], B["lk"] = lt, lk
                g = B["grp"]
                B["Rin"] = g["R"]
                if B["sb"] > 0:
                    if g["R"] is None:
                        g["R"] = (lt, lk)
                    else:
                        Rn, Rnk = wide(rpool, "rp")
                        TT("dve", Rn[:, :], g["R"][0][:, :], lt[:, :], ALU.add, g["R"][1] + lk, Rnk)
                        g["R"] = (Rn, Rnk)

            def S2a(B):
                r, c0, cs, qs, ksl = geom(B)
                pcp = 1 + (B["idx"] % 2)
                Rin = B["Rin"]
                for hh in range(2):
                    b = 2 * pcp + hh
                    MM(ps[b][:, cs], kp[buf][:, hh, ksl], qT[buf][:, qs], True, False, kkeys(B, hh), [("ps", b)])
                    if r >= 0:
                        MM(ps[b][:, c0:c0 + 128], identb, negmask[0][:, 0:128], False, False, CK, [("ps", b)])
                    MM(ps[b][:, cs], negTri, v3(B["lt"])[:, hh, cs], False, Rin is None, B["lk"] + CK, [("ps", b)])
                    if Rin is not None:
                        MM(ps[b][:, cs], negOnes, v3(Rin[0])[:, hh, cs], False, True, Rin[1] + CK, [("ps", b)])
                B["pcp"] = pcp

            def S2b(B):
                r, c0, cs, qs, ksl = geom(B)
                pcp = B["pcp"]
                a, ak = wide(bw, "bw")
                if B["bi"] == 0 and c0 > 0:
                    MEMSET("dve", v3(a)[:, :, 0:c0], 0.0, ak)
                ACT(v3(a)[:, :, cs], v3(ppair[pcp])[:, :, cs], AF.Exp, [("ps", 2 * pcp), ("ps", 2 * pcp + 1)], ak)
                B["a"], B["ak"] = a, ak

            def S3(B):
                r, c0, cs, qs, ksl = geom(B)
                qt, sb, bi, nblk = B["qt"], B["sb"], B["bi"], B["nblk"]
                if bi == 0:
                    cs = slice(0, 512)
                for hh in range(2):
                    MM(ps[6 + hh][:, cs], vv[buf][:, sb, :], v3(B["a"])[:, hh, cs], bi == 0, bi == nblk - 1,
                       B["ak"] + [("v", buf, sb // 4)], [("ps", 6 + hh)])
                if bi == nblk - 1:
                    for hh in range(2):
                        po = hh * 64
                        COPY("dve", oT[po:po + 64, hp, gsl(qt)], ps[6 + hh][po:po + 64, :], [("ps", 6 + hh)],
                             [("oT", hp, qt, hh)])

            n = len(blocks)
            for i in range(n + 3):
                if i < n:
                    S1a(blocks[i])
                if 0 <= i - 1 < n:
                    S2a(blocks[i - 1])
                if 0 <= i - 2 < n:
                    S2b(blocks[i - 2])
                if i < n:
                    S1b(blocks[i])
                if 0 <= i - 3 < n:
                    S3(blocks[i - 3])

        a_qkv(0, 0)
        for hp in range(4):
            if hp + 1 < 4:
                a_qkv(hp + 1, (hp + 1) % 2)
            attn(hp, hp % 2)

        sch.barrier(["mg"], old_names=("qT", "kp", "v"))
        woc = T["d_w_conv_o"][l].rearrange("(kc p) c -> p kc c", p=128)
        woa = T["d_w_attn_o"][l].rearrange("(kc p) c -> p kc c", p=128)
        for dc in range(8):
            wgc, wgck = ws.next(winv[:, :, 3072 + dc * 128:3072 + (dc + 1) * 128], 8)
            wga, wgak = ws.next(winv[:, :, 4096 + dc * 128:4096 + (dc + 1) * 128], 8)
            wco, wcok = ws.next(woc[:, :, dc * 128:(dc + 1) * 128], 4)
            wao, waok = ws.next(woa[:, :, dc * 128:(dc + 1) * 128], 4)
            for g in range(NG):
                pgc, pga, pcb_, pab = psalloc(), psalloc(), psalloc(), psalloc()
                for kc in range(8):
                    MM(ps[pgc][:], wgc[:, kc, :], hT[:, kc, gsl(g)], kc == 0, kc == 7, [wgck, ("hT", kc, g)], [("ps", pgc)])
                for kc in range(8):
                    MM(ps[pga][:], wga[:, kc, :], hT[:, kc, gsl(g)], kc == 0, kc == 7, [wgak, ("hT", kc, g)], [("ps", pga)])
                for j in range(4):
                    MM(ps[pcb_][:], wco[:, j, :], cy[:, j, gsl(g)], j == 0, j == 3, [wcok, ("cy", j, g)], [("ps", pcb_)])
                for j in range(4):
                    MM(ps[pab][:], wao[:, j, :], oT[:, j, gsl(g)], j == 0, j == 3,
                       [waok, ("oT", j, g, 0), ("oT", j, g, 1)], [("ps", pab)])
                gc, gck = w32()
                ga, gak = w32()
                ACT(gc[:], ps[pgc][:], AF.Sigmoid, [("ps", pgc), ("vecs",)], [gck], bias=vecs[:, bgc + dc:bgc + dc + 1])
                ACT(ga[:], ps[pga][:], AF.Sigmoid, [("ps", pga), ("vecs",)], [gak], bias=vecs[:, bgc + 8 + dc:bgc + 8 + dc + 1])
                TT("dve", gc[:], gc[:], ps[pcb_][:], ALU.mult, [gck, ("ps", pcb_)], [gck])
                TT("dve", ga[:], ga[:], ps[pab][:], ALU.mult, [gak, ("ps", pab)], [gak])
                TT("dve", mg[:, dc, gsl(g)], gc[:], ga[:], ALU.add, [gck, gak], [("mg", dc, g)])
        wov = T["d_w_out"][l].rearrange("(kc p) c -> p kc c", p=128)
        st["psmod"] = 4
        for dc in range(8):
            wo, wok = ws.next(wov[:, :, dc * 128:(dc + 1) * 128], 8)
            for g in range(NG):
                po = psalloc()
                for kc in range(8):
                    MM(ps[po][:], wo[:, kc, :], mg[:, kc, gsl(g)], kc == 0, kc == 7, [wok, ("mg", kc, g)], [("ps", po)])
                TT("dve", xT[:, dc, gsl(g)], ps[po][:], xT[:, dc, gsl(g)], ALU.add, [("ps", po)] + xk([dc], g), xk([dc], g))
                stats_after_update(dc, g, next_gcol)
                tick()
        flush()

    phase0()
    for l in range(depth):
        vb = l * 52
        ffn(T["d_ffn1_w13"][l], T["d_ffn1_w2"][l], vb + 8)
        mixer(l, vb, vb + 16)
        ffn(T["d_ffn2_w13"][l], T["d_ffn2_w2"][l], (l + 1) * 52 if l + 1 < depth else None)

    sch.barrier(["yT", "os"])
    st["psmod"] = 4
    yT, osb = T["yT"], T["os"]
    fcol = depth * 52
    def fin_norm(g):
        rt, rk = rms_stats(g, True)
        yb = yT[g % 2]
        for kc in range(8):
            STT("dve", yb[:, kc, :], xT[:, kc, gsl(g)], vecs[:, fcol + kc:fcol + kc + 1], rt[:], ALU.mult, ALU.mult,
                xk([kc], g) + [rk, ("vecs",)], [("yT", g % 2, kc)])

    fin_norm(0)
    for g in range(NG):
        if g + 1 < NG:
            fin_norm(g + 1)
        yb = yT[g % 2]
        for ti in range(4):
            tt = 4 * g + ti
            so = osb[tt % 2]
            for half in range(2):
                b = psalloc()
                for i in range(4):
                    kc = half * 4 + i
                    MM(ps[b][:, i * 128:(i + 1) * 128], yb[:, kc, ti * 128:(ti + 1) * 128], ident[:], True, True,
                       [("yT", g % 2, kc), ("ident",)], [("ps", b)])
                COPY("act", so[:, half * 512:(half + 1) * 512], ps[b][:], [("ps", b)], [("os", tt % 2, half)])
            dst = T["d_out"][tt * 128:(tt + 1) * 128, :]
            sch.op("sp", lambda so=so, dst=dst: nc.sync.dma_start(out=dst, in_=so[:]),
                   [("os", tt % 2, 0), ("os", tt % 2, 1)], [], dma=("os", tt % 2))


def build(S=2048, depth=2):
    nc = bass.Bass("TRN2", target_bir_lowering=False)
    T = {}

    def din(name, shape):
        return nc.dram_tensor(name, list(shape), F32, kind="ExternalInput").ap()

    T["d_x"] = din("x", [S, D])
    T["d_ffn1_w13"] = din("ffn1_w13", [depth, D, 2 * DFF])
    T["d_ffn1_w2"] = din("ffn1_w2", [depth, DFF, D])
    T["d_w_in"] = din("w_in", [depth, D, 5120])
    T["d_w_conv_o"] = din("w_conv_o", [depth, 512, D])
    T["d_w_attn_o"] = din("w_attn_o", [depth, 512, D])
    T["d_w_out"] = din("w_out", [depth, D, D])
    T["d_ffn2_w13"] = din("ffn2_w13", [depth, D, 2 * DFF])
    T["d_ffn2_w2"] = din("ffn2_w2", [depth, DFF, D])
    NV = depth * 52 + 8
    T["d_vecs"] = din("vecs", [128, NV])
    T["d_ident"] = din("ident", [128, 128])
    T["d_cbf"] = din("cbf", [128, 768])
    T["d_out"] = nc.dram_tensor("out", [S, D], F32, kind="ExternalOutput").ap()

    NT = S // 128
    with ExitStack() as stack:
        def sb(name, shape, dt):
            return stack.enter_context(nc.sbuf_tensor(name, list(shape), dt))

        T["xT"] = sb("xT", [128, 8, S], F32)
        T["hT"] = sb("hT", [128, 8, S], BF16)
        bigw = max(NFH * S * 2, 32 * S, 40960) // 4
        big = sb("big", [128, bigw], F32)
        bigb = big.bitcast(BF16)

        def bview(off_bytes, shape):
            n = int(np.prod(shape[1:]))
            o = off_bytes // 2
            ap = bigb[:, o:o + n]
            if len(shape) == 3:
                ap = ap.rearrange("p (a b) -> p a b", a=shape[1])
            return ap

        def fview(off_bytes, shape):
            n = int(np.prod(shape[1:]))
            o = off_bytes // 4
            ap = big[:, o:o + n]
            if len(shape) == 3:
                ap = ap.rearrange("p (a b) -> p a b", a=shape[1])
            return ap

        T["aT"] = bview(0, [128, NFH, S])
        per = 2 * S + 4 * S + 2 * S
        T["qT"] = [bview(b * per, [128, S]) for b in range(2)]
        T["kp"] = [bview(b * per + 2 * S, [128, 2, S]) for b in range(2)]
        T["v"] = [bview(b * per + 6 * S, [128, NT, 128]) for b in range(2)]
        T["mg"] = bview(0, [128, 8, S])
        T["cy"] = bview(16 * S, [128, 4, S])
        T["oT"] = bview(24 * S, [128, 4, S])
        T["xs"] = [fview(b * 4096, [128, 1024]) for b in range(6)]
        T["yT"] = [fview(b * 16384, [128, 8, 512]) for b in range(2)]
        T["os"] = [fview(32768 + b * 4096, [128, 1024]) for b in range(2)]

        T["ew"] = [sb(f"ew_{i}", [128, 1024], F32) for i in range(3)]
        T["bw"] = [sb(f"bw_{i}", [128, 1024], BF16) for i in range(5)]
        T["rp"] = [sb(f"rp_{i}", [128, 1024], BF16) for i in range(3)]
        T["ub"] = [sb(f"ub{i}", [128, 514], F32) for i in range(2)]
        slots = [sb(f"wslot{i}", [128, 8, 128], BF16) for i in range(NSLOT)]
        T["vecs"] = sb("vecs_sb", [128, NV], F32)
        T["ident"] = sb("ident_sb", [128, 128], F32)
        T["cbf"] = sb("cbf_sb", [128, 768], BF16)
        T["epsb"] = sb("epsb", [128, 1], F32)
        T["oneb"] = sb("oneb", [128, 1], F32)
        T["pp"] = [stack.enter_context(nc.psum_tensor(f"pp{i}", [128, 1024], F32)) for i in range(4)]
        T["ps"] = [T["pp"][i // 2][:, (i % 2) * 512:(i % 2 + 1) * 512] for i in range(8)]

        dry = Sched(nc, dry=True)
        rec = WStream(nc, dry, slots)
        emit_program(nc, T, dry, rec, S, depth)
        sch = Sched(nc)
        sch.op("dve", lambda: nc.vector.memset(T["epsb"][:], EPS), (), [("epsb",)])
        sch.op("dve", lambda: nc.vector.memset(T["oneb"][:], 1.0), (), [("oneb",)])
        ws = WStream(nc, sch, slots, order=rec.order)
        emit_program(nc, T, sch, ws, S, depth)
        n = sch.emit(stack)
    return nc, n


def host_consts():
    ident = np.eye(128, dtype=np.float32)
    j = np.arange(128)[:, None]
    s = np.arange(128)[None, :]
    negtri = np.where(j >= s, -1.0, 0.0).astype(np.float32)
    negones = -np.ones((128, 128), np.float32)
    onesd = np.full((128, 128), 1.0 / D, np.float32)
    i = np.arange(128)[:, None]
    c = np.arange(128)[None, :]
    mask = np.where(c <= i, NEG, 0.0).astype(np.float32)
    cbf = np.concatenate([negtri, negones, onesd, ident, mask, np.zeros((128, 128), np.float32)], axis=1)
    return ident, np.ascontiguousarray(cbf)


def pack_vecs(inp, depth):
    cols = []

    def pm(v):
        return np.asarray(v, np.float32).reshape(-1, 128).T

    for l in range(depth):
        cols.append(pm(inp["ffn1_norm"][l]))
        cols.append(pm(inp["mix_norm"][l]))
        cols.append(pm(inp["ffn2_norm"][l]))
        cols.append(pm(inp["b_gate"][l]))
        cw = np.asarray(inp["conv_w"][l], np.float32)
        cols.append(np.concatenate([pm(cw[k]) for k in range(3)], axis=1))
    cols.append(pm(inp["final_norm"]))
    return np.ascontiguousarray(np.concatenate(cols, axis=1))


_CACHE = {}


def kernel(**inputs):
    x = np.asarray(inputs["x"], np.float32)
    B, S, _ = x.shape
    depth = int(np.asarray(inputs["ffn1_w13"]).shape[0])
    key = (S, depth)
    if key not in _CACHE:
        _CACHE[key] = build(S, depth)[0]
    nc = _CACHE[key]
    ident, cbf = host_consts()
    vecs = pack_vecs(inputs, depth)
    shared = {
        "ffn1_w13": np.ascontiguousarray(inputs["ffn1_w13"], np.float32),
        "ffn1_w2": np.ascontiguousarray(inputs["ffn1_w2"], np.float32),
        "w_in": np.ascontiguousarray(inputs["w_in"], np.float32),
        "w_conv_o": np.ascontiguousarray(inputs["w_conv_o"], np.float32),
        "w_attn_o": np.ascontiguousarray(inputs["w_attn_o"], np.float32),
        "w_out": np.ascontiguousarray(inputs["w_out"], np.float32),
        "ffn2_w13": np.ascontiguousarray(inputs["ffn2_w13"], np.float32),
        "ffn2_w2": np.ascontiguousarray(inputs["ffn2_w2"], np.float32),
        "vecs": vecs, "ident": ident, "cbf": cbf,
    }
    in_maps = []
    for b in range(B):
        m = dict(shared)
        m["x"] = np.ascontiguousarray(x[b])
        in_maps.append(m)
    res = run_bass_kernel_spmd(nc, in_maps, core_ids=list(range(B)))
    return np.stack([np.asarray(r["out"], np.float32) for r in res.results], axis=0)
```
